# Optimizing a Trainium2 kernel written in Bass

```python
import math
import jax, jax.numpy as jnp
from jax import lax
import numpy as np

D_MODEL = 1024
BATCH = 8
SEQ = 8192
DEPTH = 2

GRID_W = 64
HEAD_DIM = 64
ROT_DIM = HEAD_DIM // 4
ROPE_THETA = 500000.0
MIX_WIDTH = D_MODEL
A_HEADS = MIX_WIDTH // (2 * HEAD_DIM)
B_HEADS = MIX_WIDTH // (2 * HEAD_DIM)
DILATED_BRANCHES = ((128, 1), (512, 4), (2048, 16))
A_QBLOCK = 128
NA_ROWS = 8
NA_COLS = 16
NA_QCOLS = 16
LRU_WIDTH = MIX_WIDTH // 2
LRU_BLOCKS = 8
LRU_BW = LRU_WIDTH // LRU_BLOCKS
LRU_C = 8.0
CONV_WIDTH = 4
CONV_PAD = (2, 1)
HGRN_HEADS = 4
HGRN_DK = (MIX_WIDTH // 2) // HGRN_HEADS
HGRN_DV = HGRN_DK
HGRN_CHUNK = 64
MEM_LEN = 256
XA_HEADS = 4
XA_DH = D_MODEL // XA_HEADS
D_FF = 4 * D_MODEL
N_EVEN = (DEPTH + 1) // 2
N_ODD = DEPTH // 2
EPS = 1e-6
EV_IN = 3 * (A_HEADS + B_HEADS) * HEAD_DIM
HG_K = HGRN_HEADS * HGRN_DK
HG_V = HGRN_HEADS * HGRN_DV
OD_IN = 2 * LRU_WIDTH + 3 * HG_K + 2 * HG_V
OD_OUT = LRU_WIDTH + HG_V

kernel_name = 'hybrid_dilated_na_rglru_hgrn2_encoder'


def _rms_norm(x, g):
    xf = x.astype(jnp.float32)
    y = xf * lax.rsqrt(jnp.mean(xf * xf, axis=-1, keepdims=True) + EPS)
    return (y * g.astype(jnp.float32)).astype(x.dtype)


def _partial_rope(t):
    s_len = t.shape[1]
    half = ROT_DIM // 2
    inv = jnp.asarray(ROPE_THETA ** (-np.arange(half) * 2.0 / ROT_DIM), jnp.float32)
    ang = jnp.arange(s_len, dtype=jnp.float32)[:, None] * inv[None, :]
    cos = jnp.cos(ang)[None, :, None, :]
    sin = jnp.sin(ang)[None, :, None, :]
    tf = t.astype(jnp.float32)
    t1, t2, rest = tf[..., :half], tf[..., half:ROT_DIM], tf[..., ROT_DIM:]
    return jnp.concatenate([t1 * cos - t2 * sin, t2 * cos + t1 * sin, rest], axis=-1).astype(t.dtype)


def _dilated_branch(q, k, v, window, dil):
    bn, s_len, nh, dh = q.shape
    L = s_len // dil
    P = window // (2 * dil)
    qb_len = math.gcd(L, A_QBLOCK)
    nb = L // qb_len
    kl = qb_len + 2 * P

    def sub(t):
        return t.reshape(bn, L, dil, nh, dh).transpose(0, 2, 1, 3, 4)

    qs, ks, vs = sub(q), sub(k), sub(v)
    pad = ((0, 0), (0, 0), (P, P), (0, 0), (0, 0))
    idx = np.arange(nb)[:, None] * qb_len + np.arange(kl)[None, :]
    kb = jnp.pad(ks, pad)[:, :, idx]
    vb = jnp.pad(vs, pad)[:, :, idx]
    qb = qs.reshape(bn, dil, nb, qb_len, nh, dh)
    s = jnp.einsum('brnqhc,brnkhc->brnhqk', qb, kb).astype(jnp.float32)
    rel = np.arange(kl)[None, :] - P - np.arange(qb_len)[:, None]
    kpos = np.arange(nb)[:, None, None] * qb_len + np.arange(kl)[None, None, :] - P
    valid = (np.abs(rel) <= P)[None] & (kpos >= 0) & (kpos < L)
    s = jnp.where(valid[None, None, :, None], s, -jnp.inf)
    m = jnp.max(s, axis=-1, keepdims=True)
    p = jnp.exp(s - m)
    den = jnp.sum(p, axis=-1, keepdims=True)
    o = jnp.einsum('brnhqk,brnkhc->brnqhc', (p / den).astype(v.dtype), vb)
    lse = (m + jnp.log(den))[..., 0]
    o = o.reshape(bn, dil, L, nh, dh).transpose(0, 2, 1, 3, 4).reshape(bn, s_len, nh, dh)
    lse = lse.transpose(0, 1, 2, 4, 3).reshape(bn, dil, L, nh).transpose(0, 2, 1, 3).reshape(bn, s_len, nh)
    return o, lse


def _dilated_mixture_attention(q, k, v):
    outs, lses = [], []
    for window, dil in DILATED_BRANCHES:
        o, l = _dilated_branch(q, k, v, window, dil)
        outs.append(o)
        lses.append(l)
    w = jax.nn.softmax(jnp.stack(lses, axis=0), axis=0)
    o = jnp.sum(w[..., None] * jnp.stack(outs, axis=0).astype(jnp.float32), axis=0)
    return o.astype(q.dtype)


def _neighborhood_attention(q, k, v, rpb):
    bn, s_len, nh, dh = q.shape
    rows = s_len // GRID_W
    kh = min(NA_ROWS, rows)
    ncb = GRID_W // NA_QCOLS
    kcols = NA_QCOLS + NA_COLS

    def grid(t):
        return t.reshape(bn, rows, GRID_W, nh, dh)

    qg, kg, vg = grid(q), grid(k), grid(v)
    row_start = np.clip(np.arange(rows) - kh // 2, 0, rows - kh)
    row_bias = row_start[:, None] + np.arange(kh)[None, :] - np.arange(rows)[:, None] + NA_ROWS - 1
    cb_start = np.clip(np.arange(ncb) * NA_QCOLS - NA_COLS // 2, 0, GRID_W - kcols)
    key_col = cb_start[:, None] + np.arange(kcols)[None, :]
    q_col = np.arange(ncb)[:, None] * NA_QCOLS + np.arange(NA_QCOLS)[None, :]
    q_col_start = np.clip(q_col - NA_COLS // 2, 0, GRID_W - NA_COLS)
    kc = key_col[:, None, :]
    col_valid = (kc >= q_col_start[..., None]) & (kc < q_col_start[..., None] + NA_COLS)
    col_bias = np.clip(kc - q_col[..., None], 1 - NA_COLS, NA_COLS - 1) + NA_COLS - 1
    mask = col_valid[None, None, :, :, None, :]

    def one_row(args):
        q_row, r0, rb = args
        k_rows = lax.dynamic_slice_in_dim(kg, r0, kh, axis=1)[:, :, key_col]
        v_rows = lax.dynamic_slice_in_dim(vg, r0, kh, axis=1)[:, :, key_col]
        qb = q_row.reshape(bn, ncb, NA_QCOLS, nh, dh)
        s = jnp.einsum('bjuhc,bijvhc->bhjuiv', qb, k_rows).astype(jnp.float32)
        bias = rpb[:, rb[None, None, :, None], col_bias[:, :, None, :]].astype(jnp.float32)
        s = jnp.where(mask, s + bias[None], -jnp.inf)
        p = jax.nn.softmax(s.reshape(bn, nh, ncb, NA_QCOLS, kh * kcols), axis=-1)
        p = p.reshape(bn, nh, ncb, NA_QCOLS, kh, kcols).astype(v.dtype)
        o = jnp.einsum('bhjuiv,bijvhc->bjuhc', p, v_rows)
        return o.reshape(bn, GRID_W, nh, dh)

    out = lax.map(one_row, (qg.transpose(1, 0, 2, 3, 4),
                            jnp.asarray(row_start, jnp.int32),
                            jnp.asarray(row_bias, jnp.int32)))
    return out.transpose(1, 0, 2, 3, 4).reshape(bn, s_len, nh, dh)


def _even_mixer(h, w_in, w_out, rpb):
    bn, s_len, _ = h.shape
    a_w, b_w = A_HEADS * HEAD_DIM, B_HEADS * HEAD_DIM
    proj = h @ w_in
    qa, ka, va, qb, kb, vb = jnp.split(proj, np.cumsum([a_w, a_w, a_w, b_w, b_w]).tolist(), axis=-1)
    scale = HEAD_DIM ** -0.5

    def heads(t, n):
        return t.reshape(bn, s_len, n, HEAD_DIM)

    y_a = _dilated_mixture_attention(_partial_rope(heads(qa, A_HEADS)) * scale,
                                     _partial_rope(heads(ka, A_HEADS)), heads(va, A_HEADS))
    y_b = _neighborhood_attention(heads(qb, B_HEADS) * scale, heads(kb, B_HEADS), heads(vb, B_HEADS), rpb)
    y = jnp.concatenate([y_a.reshape(bn, s_len, a_w), y_b.reshape(bn, s_len, b_w)], axis=-1)
    return y @ w_out


def _depthwise_conv(u, w, b):
    c = u.shape[-1]
    y = lax.conv_general_dilated(u, w[:, None, :].astype(u.dtype), window_strides=(1,),
                                 padding=[CONV_PAD], dimension_numbers=('NWC', 'WIO', 'NWC'),
                                 feature_group_count=c)
    return y + b.astype(u.dtype)


def _lin_combine(e1, e2):
    a1, b1 = e1
    a2, b2 = e2
    return a1 * a2, a2 * b1 + b2


def _rg_lru(u, wa, ba, wx, bx, lam, reverse):
    bn, s_len, c = u.shape
    ub = u.reshape(bn, s_len, LRU_BLOCKS, LRU_BW)
    r = jax.nn.sigmoid(jnp.einsum('bsnc,ncd->bsnd', ub, wa.astype(jnp.float32)).reshape(bn, s_len, c)
                       + ba.astype(jnp.float32))
    i = jax.nn.sigmoid(jnp.einsum('bsnc,ncd->bsnd', ub, wx.astype(jnp.float32)).reshape(bn, s_len, c)
                       + bx.astype(jnp.float32))
    log_a = -LRU_C * r * jax.nn.softplus(-lam.astype(jnp.float32))
    a = jnp.exp(log_a)
    b = jnp.sqrt(-jnp.expm1(2.0 * log_a)) * (i * u)
    _, hs = lax.associative_scan(_lin_combine, (a, b), reverse=reverse, axis=1)
    return hs


def _gla_chunk_scan(q, k, v, logf):
    bn, s_len, nh, dk = q.shape
    dv = v.shape[-1]
    cl = math.gcd(s_len, HGRN_CHUNK)
    nc = s_len // cl

    def chunks(t):
        return t.reshape(bn, nc, cl, nh, t.shape[-1]).transpose(1, 0, 3, 2, 4)

    tri = np.tril(np.ones((cl, cl), dtype=bool))[:, :, None]

    def step(state, xs):
        qc, kc, vc, gc = xs
        G = jnp.cumsum(gc, axis=2)
        o_inter = jnp.einsum('bhtd,bhdv->bhtv', qc * jnp.exp(G), state)
        diff = G[:, :, :, None, :] - G[:, :, None, :, :]
        decay = jnp.exp(jnp.where(tri, diff, -jnp.inf))
        att = jnp.einsum('bhtd,bhsd,bhtsd->bhts', qc, kc, decay)
        o_intra = jnp.einsum('bhts,bhsv->bhtv', att, vc)
        g_last = G[:, :, -1:, :]
        new_state = (state * jnp.exp(g_last[:, :, 0, :, None])
                     + jnp.einsum('bhsd,bhsv->bhdv', kc * jnp.exp(g_last - G), vc))
        return new_state, o_inter + o_intra

    init = jnp.zeros((bn, nh, dk, dv), jnp.float32)
    _, o = lax.scan(step, init, (chunks(q), chunks(k), chunks(v), chunks(logf)))
    return o.transpose(1, 0, 3, 2, 4).reshape(bn, s_len, nh, dv)


def _hgrn2_gates(f_logit, lb):
    bn, s_len, _ = f_logit.shape
    fl = f_logit.astype(jnp.float32).reshape(bn, s_len, HGRN_HEADS, HGRN_DK)
    lb = lb.astype(jnp.float32).reshape(HGRN_HEADS, HGRN_DK)
    log_f = jnp.logaddexp(jnp.log(lb), jnp.log1p(-lb) + jax.nn.log_sigmoid(fl))
    key = (1.0 - lb) * jax.nn.sigmoid(-fl)
    return key, log_f


def _odd_mixer(h, w_in, w_out, conv_w, conv_b, wa, ba, wx, bx, lam, lb_f, lb_b, gnorm_g):
    bn, s_len, _ = h.shape
    proj = h @ w_in
    sizes = (LRU_WIDTH, LRU_WIDTH, HG_K, HG_K, HG_K, HG_V)
    u, gate, q, f_fw, f_bw, i_in, g_out = jnp.split(proj, np.cumsum(sizes).tolist(), axis=-1)
    u = _depthwise_conv(u, conv_w, conv_b).astype(jnp.float32)
    h_fw = _rg_lru(u, wa[0], ba[0], wx[0], bx[0], lam[0], False)
    h_bw = _rg_lru(u, wa[1], ba[1], wx[1], bx[1], lam[1], True)
    y_c = (h_fw + h_bw) * jax.nn.gelu(gate.astype(jnp.float32))
    qh = jax.nn.silu(q.astype(jnp.float32)).reshape(bn, s_len, HGRN_HEADS, HGRN_DK)
    vh = i_in.astype(jnp.float32).reshape(bn, s_len, HGRN_HEADS, HGRN_DV)
    k_fw, lf_fw = _hgrn2_gates(f_fw, lb_f)
    k_bw, lf_bw = _hgrn2_gates(f_bw, lb_b)

    def flip(t):
        return jnp.flip(t, axis=1)

    o = (_gla_chunk_scan(qh, k_fw, vh, lf_fw)
         + flip(_gla_chunk_scan(flip(qh), flip(k_bw), flip(vh), flip(lf_bw))))
    o = o * lax.rsqrt(jnp.mean(o * o, axis=-1, keepdims=True) + EPS) * gnorm_g.astype(jnp.float32)
    y_d = o.reshape(bn, s_len, HG_V) * jax.nn.silu(g_out.astype(jnp.float32))
    y = jnp.concatenate([y_c, y_d], axis=-1).astype(h.dtype)
    return y @ w_out


def _memory_cross_attention(h, mem_n, wq, wkv, wo):
    bn, s_len, _ = h.shape
    m_len = mem_n.shape[1]
    q = (h @ wq).reshape(bn, s_len, XA_HEADS, XA_DH) * (XA_DH ** -0.5)
    k, v = jnp.split(mem_n @ wkv, 2, axis=-1)
    k = k.reshape(bn, m_len, XA_HEADS, XA_DH)
    v = v.reshape(bn, m_len, XA_HEADS, XA_DH)
    p = jax.nn.softmax(jnp.einsum('bshc,bmhc->bhsm', q, k).astype(jnp.float32), axis=-1)
    o = jnp.einsum('bhsm,bmhc->bshc', p.astype(v.dtype), v).reshape(bn, s_len, D_MODEL)
    return o @ wo


def _sq_relu_mlp(h, w1, w2):
    return jnp.square(jax.nn.relu(h @ w1)) @ w2


def setup_inputs(seed: int = 0) -> dict:
    key = jax.random.key(seed)
    ks = jax.random.split(key, 32)
    f32 = jnp.float32

    def w(k, shape, fan_in):
        return jax.random.normal(k, shape, f32) * (fan_in ** -0.5)

    def gain(k, shape):
        return 1.0 + 0.05 * jax.random.normal(k, shape, f32)

    def small(k, shape, s):
        return s * jax.random.normal(k, shape, f32)

    a0 = jax.random.uniform(ks[18], (N_ODD, 2, LRU_WIDTH), f32, 0.9, 0.999)
    s0 = a0 ** (1.0 / LRU_C)
    lru_lambda = jnp.log(s0) - jnp.log1p(-s0)
    return {
        'x': jax.random.normal(ks[0], (BATCH, SEQ, D_MODEL), f32),
        'mem': jax.random.normal(ks[1], (BATCH, MEM_LEN, D_MODEL), f32),
        'norm_mix_g': gain(ks[2], (DEPTH, D_MODEL)),
        'norm_xa_g': gain(ks[3], (DEPTH, D_MODEL)),
        'norm_mem_g': gain(ks[4], (DEPTH, D_MODEL)),
        'norm_mlp_g': gain(ks[5], (DEPTH, D_MODEL)),
        'final_norm_g': gain(ks[6], (D_MODEL,)),
        'ev_w_in': w(ks[7], (N_EVEN, D_MODEL, EV_IN), D_MODEL),
        'ev_w_out': w(ks[8], (N_EVEN, (A_HEADS + B_HEADS) * HEAD_DIM, D_MODEL), (A_HEADS + B_HEADS) * HEAD_DIM),
        'na_rpb': small(ks[9], (N_EVEN, B_HEADS, 2 * NA_ROWS - 1, 2 * NA_COLS - 1), 0.2),
        'od_w_in': w(ks[10], (N_ODD, D_MODEL, OD_IN), D_MODEL),
        'od_w_out': w(ks[11], (N_ODD, OD_OUT, D_MODEL), OD_OUT),
        'conv_w': w(ks[12], (N_ODD, CONV_WIDTH, LRU_WIDTH), CONV_WIDTH),
        'conv_b': small(ks[13], (N_ODD, LRU_WIDTH), 0.02),
        'lru_wa': w(ks[14], (N_ODD, 2, LRU_BLOCKS, LRU_BW, LRU_BW), LRU_BW),
        'lru_ba': small(ks[15], (N_ODD, 2, LRU_WIDTH), 0.1),
        'lru_wx': w(ks[16], (N_ODD, 2, LRU_BLOCKS, LRU_BW, LRU_BW), LRU_BW),
        'lru_bx': small(ks[17], (N_ODD, 2, LRU_WIDTH), 0.1),
        'lru_lambda': lru_lambda,
        'hgrn_lb_logits': small(ks[19], (DEPTH, 2, HG_K), 0.5),
        'hgrn_norm_g': gain(ks[20], (N_ODD, HGRN_DV)),
        'xa_wq': w(ks[21], (DEPTH, D_MODEL, XA_HEADS * XA_DH), D_MODEL),
        'xa_wkv': w(ks[22], (DEPTH, D_MODEL, 2 * XA_HEADS * XA_DH), D_MODEL),
        'xa_wo': w(ks[23], (DEPTH, XA_HEADS * XA_DH, D_MODEL), XA_HEADS * XA_DH),
        'mlp_w1': w(ks[24], (DEPTH, D_MODEL, D_FF), D_MODEL),
        'mlp_w2': w(ks[25], (DEPTH, D_FF, D_MODEL), D_FF),
    }


def reference(x, mem, norm_mix_g, norm_xa_g, norm_mem_g, norm_mlp_g, final_norm_g,
              ev_w_in, ev_w_out, na_rpb, od_w_in, od_w_out, conv_w, conv_b,
              lru_wa, lru_ba, lru_wx, lru_bx, lru_lambda, hgrn_lb_logits, hgrn_norm_g,
              xa_wq, xa_wkv, xa_wo, mlp_w1, mlp_w2):
    p_lb = jax.nn.softmax(hgrn_lb_logits.astype(jnp.float32), axis=0)
    lower_bounds = jnp.cumsum(p_lb, axis=0) - p_lb[0:1]
    for layer in range(DEPTH):
        h = _rms_norm(x, norm_mix_g[layer])
        if layer % 2 == 0:
            e = layer // 2
            x = x + _even_mixer(h, ev_w_in[e], ev_w_out[e], na_rpb[e])
        else:
            o = layer // 2
            x = x + _odd_mixer(h, od_w_in[o], od_w_out[o], conv_w[o], conv_b[o],
                               lru_wa[o], lru_ba[o], lru_wx[o], lru_bx[o], lru_lambda[o],
                               lower_bounds[layer, 0], lower_bounds[layer, 1], hgrn_norm_g[o])
        x = x + _memory_cross_attention(_rms_norm(x, norm_xa_g[layer]), _rms_norm(mem, norm_mem_g[layer]),
                                        xa_wq[layer], xa_wkv[layer], xa_wo[layer])
        x = x + _sq_relu_mlp(_rms_norm(x, norm_mlp_g[layer]), mlp_w1[layer], mlp_w2[layer])
    return _rms_norm(x, final_norm_g)
```

```python
import math
from contextlib import ExitStack
import numpy as np
import concourse.bass as bass
import concourse.mybir as mybir
from concourse.bass_utils import run_bass_kernel_spmd

F32 = mybir.dt.float32
BF16 = mybir.dt.bfloat16
AF = mybir.ActivationFunctionType
ALU = mybir.AluOpType

ENGS = ['sync', 'scalar', 'vector', 'gpsimd', 'tensor']
S = 8192
D = 1024
TT = 512
NT = S // TT
EPS = 1e-6


class _Op:
    __slots__ = ('eng', 'fn', 'waits', 'ordinal', 'dma')


class Prog:
    def __init__(self, nc):
        self.nc = nc
        self.esem = {e: nc.alloc_semaphore(f"cnt_{e}") for e in ENGS}
        self.base = {e: 0 for e in ENGS}
        self.dcnt = {}
        self.dsems = {}
        self.nt = 0
        self.stack = None
        self._reset()

    def _reset(self):
        self.streams = {e: [] for e in ENGS}
        self.nord = {e: 0 for e in ENGS}
        self.seen = {e: {} for e in ENGS}
        self.bufs = {}

    def begin(self):
        self.stack = ExitStack()
        self._reset()

    def sb(self, shape, dtype, name=None):
        self.nt += 1
        return self.stack.enter_context(self.nc.sbuf_tensor(name or f"t{self.nt}", list(shape), dtype))

    def ps(self, shape, dtype=F32, name=None):
        self.nt += 1
        return self.stack.enter_context(self.nc.psum_tensor(name or f"p{self.nt}", list(shape), dtype))

    def _buf(self, k):
        b = self.bufs.get(k)
        if b is None:
            b = self.bufs[k] = [{}, {}, {}, None]
        return b

    def op(self, eng, fn, reads=(), writes=(), dma=None, group=None, deps=()):
        if group is False:
            group = None
        waits = {}

        def need(d):
            for s, v in d.items():
                if v > waits.get(s, 0):
                    waits[s] = v
        for k in deps:
            need(self._buf(k)[0])
        for k in reads:
            need(self._buf(k)[0])
        joins = []
        for k in writes:
            b = self._buf(k)
            j = group is not None and b[3] == group and not b[1]
            joins.append(j)
            if j:
                need(b[2])
            else:
                need(b[0])
                need(b[1])
        seen = self.seen[eng]
        fw = {}
        for s, v in waits.items():
            if s == ('e', eng) and eng == 'tensor':
                continue
            if v > seen.get(s, 0):
                seen[s] = v
                fw[s] = v
        o = _Op()
        o.eng = eng
        o.fn = fn
        o.waits = fw
        o.dma = dma
        if fn is None:
            o.ordinal = None
            self.streams[eng].append(o)
            return o
        if dma is None:
            self.nord[eng] += 1
            o.ordinal = self.nord[eng]
            ev = (('e', eng), o.ordinal)
        else:
            if dma not in self.dsems:
                self.dsems[dma] = self.nc.alloc_semaphore(f"d{len(self.dsems)}")
                self.dcnt[dma] = 0
            self.dcnt[dma] += 16
            o.ordinal = None
            ev = (('d', dma), self.dcnt[dma])
        self.streams[eng].append(o)
        for k in reads:
            b = self._buf(k)
            if ev[1] > b[1].get(ev[0], 0):
                b[1][ev[0]] = ev[1]
        for k, j in zip(writes, joins):
            b = self._buf(k)
            if j:
                b[0][ev[0]] = max(ev[1], b[0].get(ev[0], 0))
            else:
                if group is not None:
                    pre = dict(b[0])
                    for s_, v_ in b[1].items():
                        if v_ > pre.get(s_, 0):
                            pre[s_] = v_
                    b[2] = pre
                else:
                    b[2] = {}
                b[0] = {ev[0]: ev[1]}
                b[1] = {}
                b[3] = group
        return o

    def _simulate(self, rank):
        val = {}
        for e in ENGS:
            val[('e', e)] = self.base[e]
        pos = {e: 0 for e in ENGS}
        dval = getattr(self, '_dval', {})
        progress = True
        while progress:
            progress = False
            for e in ENGS:
                st = self.streams[e]
                while pos[e] < len(st):
                    o = st[pos[e]]
                    ok = True
                    for s_, v in o.waits.items():
                        if s_[0] == 'e':
                            if val[s_] < rank[s_[1]][v]:
                                ok = False
                                break
                        elif dval.get(s_[1], 0) < v:
                            ok = False
                            break
                    if not ok:
                        break
                    if o.fn is not None:
                        if o.dma is not None:
                            dval[o.dma] = dval.get(o.dma, 0) + 16
                        elif o.ordinal in rank[e]:
                            val[('e', e)] += 1
                    pos[e] += 1
                    progress = True
        self._dval = dval
        stuck = {e: pos[e] for e in ENGS if pos[e] < len(self.streams[e])}
        if stuck:
            raise RuntimeError(f"sync deadlock detected at build time: {stuck}")

    def end(self):
        nc = self.nc
        o = _Op()
        o.eng = 'sync'
        o.fn = None
        o.dma = None
        o.ordinal = None
        o.waits = {('d', k): v for k, v in self.dcnt.items() if v > self.seen['sync'].get(('d', k), 0)}
        self.streams['sync'].append(o)
        marked = {e: set() for e in ENGS}
        for e in ENGS:
            for op_ in self.streams[e]:
                for s, v in op_.waits.items():
                    if s[0] == 'e':
                        marked[s[1]].add(v)
        rank = {}
        for e in ENGS:
            rank[e] = {v: self.base[e] + i + 1 for i, v in enumerate(sorted(marked[e]))}
        streams = self.streams
        esem = self.esem
        dsems = self.dsems
        self._simulate(rank)

        def body_for(e):
            def body(eng):
                for op_ in streams[e]:
                    for s, v in op_.waits.items():
                        if s[0] == 'e':
                            eng.wait_ge(esem[s[1]], rank[s[1]][v])
                        else:
                            eng.wait_ge(dsems[s[1]], v)
                    if op_.fn is None:
                        continue
                    ins = op_.fn(eng)
                    if op_.dma is not None:
                        ins.then_inc(dsems[op_.dma], 16)
                    elif op_.ordinal in rank[e]:
                        ins.then_inc(esem[e], 1)
            return body
        with nc.Block() as block:
            block.sync(body_for('sync'))
            block.scalar(body_for('scalar'))
            block.vector(body_for('vector'))
            block.gpsimd(body_for('gpsimd'))
            block.tensor(body_for('tensor'))
        n = {e: len(streams[e]) for e in ENGS}
        for e in ENGS:
            self.base[e] += len(marked[e])
        self.stack.close()
        self.stack = None
        self._reset()
        return n


CV = {}


def _cv_layout():
    off = 0
    for name, n in [('mix_g0', 8), ('xa_g0', 8), ('mem_g0', 8), ('mlp_g0', 8),
                    ('mix_g1', 8), ('xa_g1', 8), ('mem_g1', 8), ('mlp_g1', 8), ('fin_g', 8),
                    ('conv_w', 16), ('conv_b', 4), ('lru_ba', 8), ('lru_bx', 8), ('lru_lam', 8),
                    ('lb_logit', 16), ('gnorm', 1)]:
        CV[name] = (off, n)
        off += n
    return off


NCV = _cv_layout()


def _colmajor(v):
    v = np.asarray(v, np.float32).reshape(-1, 128)
    return v.T


def pack_cvec(inp):
    cv = np.zeros((128, NCV), np.float32)

    def put(name, arr):
        o, n = CV[name]
        assert arr.shape == (128, n), (name, arr.shape)
        cv[:, o:o + n] = arr
    for l in range(2):
        put(f'mix_g{l}', _colmajor(inp['norm_mix_g'][l]))
        put(f'xa_g{l}', _colmajor(inp['norm_xa_g'][l]))
        put(f'mem_g{l}', _colmajor(inp['norm_mem_g'][l]))
        put(f'mlp_g{l}', _colmajor(inp['norm_mlp_g'][l]))
    put('fin_g', _colmajor(inp['final_norm_g']))
    put('conv_w', np.concatenate([_colmajor(inp['conv_w'][0, j]) for j in range(4)], axis=1))
    put('conv_b', _colmajor(inp['conv_b'][0]))
    put('lru_ba', np.concatenate([_colmajor(inp['lru_ba'][0, d_]) for d_ in range(2)], axis=1))
    put('lru_bx', np.concatenate([_colmajor(inp['lru_bx'][0, d_]) for d_ in range(2)], axis=1))
    put('lru_lam', np.concatenate([_colmajor(inp['lru_lambda'][0, d_]) for d_ in range(2)], axis=1))
    put('lb_logit', np.concatenate([_colmajor(inp['hgrn_lb_logits'][l, d_]) for l in range(2) for d_ in range(2)], axis=1))
    put('gnorm', np.asarray(inp['hgrn_norm_g'][0], np.float32).reshape(128, 1))
    return cv


class Builder:
    def __init__(self, ext_in=(), ext_out=()):
        self.nc = bass.Bass("TRN2", target_bir_lowering=False)
        self.P = Prog(self.nc)
        self.ext_in = set(ext_in)
        self.ext_out = set(ext_out)
        self.dr = {}
        self.psn = 0

    def dram(self, name, shape, dtype, kind=None):
        if name in self.dr:
            return self.dr[name]
        if kind is None:
            kind = "ExternalInput" if name in self.ext_in else ("ExternalOutput" if name in self.ext_out else "Internal")
        t = self.nc.dram_tensor(name, list(shape), dtype, kind=kind).ap()
        self.dr[name] = t
        return t

    def cast_weight(self, src, dst, K, F, key):
        P = self.P
        fc = min(F, 512)
        for a in range(K // 128):
            s = src[a * 128:(a + 1) * 128, :].rearrange("p (c f) -> p c f", f=fc)
            d = dst[a * 128:(a + 1) * 128, :].rearrange("p (c f) -> p c f", f=fc)
            P.op('gpsimd', lambda e, s=s, d=d: e.dma_start(out=d, in_=s), writes=[key], dma=key, group=True)


def wview(w, K):
    return w.rearrange("(a p) f -> p a f", p=128)


def _phase_common(self, nring=5, nps=8):
    P = self.P
    P.begin()
    self.pst = [P.ps([128, 512], F32) for _ in range(nps)]
    self.nps = nps
    self.psi = 0
    self.wring = [P.sb([128, 8, 512], BF16) for _ in range(nring)]
    self.wri = 0
    self.cv = P.sb([128, NCV], F32)
    cvd = self.dram('cvec', [128, NCV], F32, kind="ExternalInput")
    P.op('sync', lambda e: e.dma_start(out=self.cv[:], in_=cvd), writes=['cv'], dma='cv')
    self.ones32 = P.sb([128, 128], F32)
    self.onesb = P.sb([128, 128], BF16)
    P.op('vector', lambda e: e.memset(self.ones32[:], 1.0), writes=['ones32'])
    P.op('vector', lambda e: e.memset(self.onesb[:], 1.0), writes=['onesb'])


def _next_ps(self):
    i = self.psi % self.nps
    self.psi += 1
    return self.pst[i], ('ps', i)


def _load_w(self, wd, kb, fb, nk=8, wkey=None):
    P = self.P
    i = self.wri % len(self.wring)
    self.wri += 1
    t = self.wring[i]
    src = wview(wd, None)[:, kb * 8:kb * 8 + nk, fb * 512:(fb + 1) * 512]
    P.op('sync', lambda e, t=t, src=src: e.dma_start(out=t[:, 0:nk, :], in_=src),
         reads=[wkey] if wkey else [], writes=[('wr', i)], dma=('wr', i))
    return t, ('wr', i)


def _linear(self, wd, nkb, nfb, rhs, rkeys, evac, wkey=None, T=TT, nk=8, fbs=None):
    P = self.P
    for fb in (fbs if fbs is not None else range(nfb)):
        pss = [self.next_ps() for _ in range(4)]
        for kb in range(nkb):
            wt, wk = self.load_w(wd, kb, fb, nk=nk, wkey=wkey)
            for f4 in range(4):
                ps, pk = pss[f4]
                for a in range(nk):
                    P.op('tensor', lambda e, ps=ps, wt=wt, a=a, f4=f4, ga=kb * 8 + a, st=(kb == 0 and a == 0), sp=(kb == nkb - 1 and a == nk - 1):
                         e.matmul(ps[:, 0:T], lhsT=wt[:, a, f4 * 128:(f4 + 1) * 128], rhs=rhs(ga), start=st, stop=sp),
                         reads=[wk] + rkeys(kb * 8 + a), writes=[pk])
        for f4 in range(4):
            ps, pk = pss[f4]
            evac(fb * 4 + f4, ps, pk)


def _rmsnorm(self, x32, xkeys, gname, hb, hkeys, sq, T=TT, out32=None):
    P = self.P
    go, _ = CV[gname]
    ps, pk = self.next_ps()
    for a in range(8):
        s = sq[a % 4]
        P.op('scalar', lambda e, s=s, a=a: e.activation(out=s[:, 0:T], in_=x32[:, a, 0:T], func=AF.Square),
             reads=[xkeys(a)], writes=[('sq', a % 4)])
        P.op('tensor', lambda e, s=s, a=a, ps=ps: e.matmul(ps[:, 0:T], lhsT=self.onesb[:], rhs=s[:, 0:T], start=(a == 0), stop=(a == 7)),
             reads=[('sq', a % 4), 'onesb'], writes=[pk])
    rstd = self.rstd
    P.op('scalar', lambda e, ps=ps: e.activation(out=rstd[:, 0:T], in_=ps[:, 0:T], func=AF.Ln, scale=1.0 / D, bias=self.epsb[:, 0:1]),
         reads=[pk, 'epsb'], writes=['rstd'])
    P.op('scalar', lambda e: e.activation(out=rstd[:, 0:T], in_=rstd[:, 0:T], func=AF.Exp, scale=-0.5), reads=['rstd'], writes=['rstd'])
    for a in range(8):
        dst = hb if out32 is None else out32
        P.op('vector', lambda e, a=a, dst=dst: e.scalar_tensor_tensor(out=dst[:, a, 0:T], in0=x32[:, a, 0:T], scalar=self.cv[:, go + a:go + a + 1],
                                                                      in1=rstd[:, 0:T], op0=ALU.mult, op1=ALU.mult),
             reads=[xkeys(a), 'rstd', 'cv'], writes=[hkeys(a)])


def _norm_bufs(self, T=TT):
    P = self.P
    self.sqb = [P.sb([128, T], BF16) for _ in range(4)]
    self.rstd = P.sb([128, T], F32)
    self.epsb = P.sb([128, 1], F32)
    P.op('vector', lambda e: e.memset(self.epsb[:], EPS), writes=['epsb'])


Builder.phase_common = _phase_common
Builder.next_ps = _next_ps
Builder.load_w = _load_w
Builder.linear = _linear
Builder.rmsnorm = _rmsnorm
Builder.norm_bufs = _norm_bufs


WSPECS = {'ev_w_in': (1, 1024, 3072), 'ev_w_out': (1, 1024, 1024), 'od_w_in': (1, 1024, 3584), 'od_w_out': (1, 1024, 1024),
          'xa_wq': (2, 1024, 1024), 'xa_wkv': (2, 1024, 2048), 'xa_wo': (2, 1024, 1024), 'mlp_w1': (2, 1024, 4096), 'mlp_w2': (2, 4096, 1024)}


def _cast_list(self, items, after=()):
    if after:
        self.P.op('gpsimd', None, reads=list(after))
    for name, l in items:
        L, K, F = WSPECS[name]
        src = self.dram(name, [L, K, F], F32, kind="ExternalInput")
        dst = self.dram(f'{name}_b{l}', [K, F], BF16)
        self.cast_weight(src[l], dst, K, F, key=f'{name}_b{l}')


Builder.cast_list = _cast_list


def _cast_queue(self, items):
    q = []
    for name, l in items:
        L, K, F = WSPECS[name]
        src = self.dram(name, [L, K, F], F32, kind="ExternalInput")
        dst = self.dram(f'{name}_b{l}', [K, F], BF16)
        fc = min(F, 512)
        for a in range(K // 128):
            sa = src[l][a * 128:(a + 1) * 128, :].rearrange("p (c f) -> p c f", f=fc)
            da = dst[a * 128:(a + 1) * 128, :].rearrange("p (c f) -> p c f", f=fc)
            q.append((sa, da, f'{name}_b{l}'))
    self.castq = q


def _emit_cast(self, after=(), n=1):
    for _ in range(n):
        if not getattr(self, 'castq', None):
            return
        sa, da, key = self.castq.pop(0)
        self.P.op('gpsimd', lambda e, sa=sa, da=da: e.dma_start(out=da, in_=sa), deps=list(after), writes=[key], dma=key, group=True)


Builder.cast_queue = _cast_queue
Builder.emit_cast = _emit_cast

def _prep_phase(self):
    P = self.P
    self.phase_common()
    self.norm_bufs(256)
    self.cast_list([('xa_wkv', 0), ('xa_wkv', 1), ('ev_w_in', 0)])
    memT = self.dram('memT', [D, 256], F32, kind="ExternalInput")
    m32 = P.sb([128, 8, 256], F32)
    P.op('sync', lambda e: e.dma_start(out=m32[:], in_=wview(memT, None)), writes=['m32'], dma='m32')
    hm = P.sb([128, 8, 256], BF16)
    kst = P.sb([128, 8, 256], BF16)
    vst = P.sb([128, 2, 1024], BF16)
    for l in range(2):
        self.rmsnorm(m32, lambda a: 'm32', f'mem_g{l}', hm, lambda a: 'hm', self.sqb, T=256)
        wkv = self.dr[f'xa_wkv_b{l}']
        kd = self.dram(f'memK{l}', [D, 256], BF16)
        vd = self.dram(f'memV{l}', [256, D], BF16)

        def evac_k(f, ps, pk):
            P.op('scalar', lambda e, f=f, ps=ps: e.activation(out=kst[:, f, :], in_=ps[:, 0:256], func=AF.Copy),
                 reads=[pk], writes=['kst'])
        self.linear(wkv, 1, 2, lambda a: hm[:, a, :], lambda a: ['hm'], evac_k, wkey=f'xa_wkv_b{l}', T=256)
        P.op('gpsimd', lambda e, kd=kd: e.dma_start(out=wview(kd, None), in_=kst[:]), reads=['kst'], writes=[f'memK{l}'], dma='kst')
        for fb in range(2):
            wt, wk = self.load_w(wkv, 0, 2 + fb, wkey=f'xa_wkv_b{l}')
            for mc in range(2):
                ps, pk = self.next_ps()
                for a in range(8):
                    P.op('tensor', lambda e, ps=ps, wt=wt, a=a, mc=mc: e.matmul(ps[:], lhsT=hm[:, a, mc * 128:(mc + 1) * 128], rhs=wt[:, a, :],
                                                                               start=(a == 0), stop=(a == 7)),
                         reads=[wk, 'hm'], writes=[pk])
                P.op('vector', lambda e, ps=ps, mc=mc, fb=fb: e.tensor_copy(out=vst[:, mc, fb * 512:(fb + 1) * 512], in_=ps[:]),
                     reads=[pk], writes=['vst'])
        P.op('gpsimd', lambda e, vd=vd: e.dma_start(out=vd.rearrange("(c p) f -> p c f", p=128), in_=vst[:]),
             reads=['vst'], writes=[f'memV{l}'], dma='vst')
    return P.end()


Builder.prep_phase = _prep_phase


def _dense_phase(self, layer, x_in, y_in, x_out, final, ntiles=NT):
    P = self.P
    self.phase_common(nring=6)
    self.norm_bufs()
    l = layer
    wo_mix = self.dr['ev_w_out_b0' if l == 0 else 'od_w_out_b0']
    wq = self.dr[f'xa_wq_b{l}']
    wo = self.dr[f'xa_wo_b{l}']
    w1 = self.dr[f'mlp_w1_b{l}']
    w2 = self.dr[f'mlp_w2_b{l}']
    xin = self.dram(x_in, [D, S], F32)
    yin = self.dram(y_in, [D, S], BF16)
    xout = self.dram(x_out, [D, S], F32)
    memK = P.sb([128, 8, 256], BF16)
    memV = P.sb([128, 2, 1024], BF16)
    P.op('sync', lambda e: e.dma_start(out=memK[:], in_=wview(self.dr[f'memK{l}'], None)), writes=['memK'], dma='memK')
    P.op('sync', lambda e: e.dma_start(out=memV[:], in_=self.dr[f'memV{l}'].rearrange("(c p) f -> p c f", p=128)), writes=['memV'], dma='memV')
    x32s = [P.sb([128, 8, TT], F32) for _ in range(2)]
    ybs = [P.sb([128, 8, TT], BF16) for _ in range(2)]
    hb = P.sb([128, 8, TT], BF16)
    qb = P.sb([128, 8, TT], BF16)
    ob = P.sb([128, 8, TT], BF16)
    h1 = P.sb([128, 32, TT], BF16)
    pTs = [P.sb([128, 2, TT], BF16) for _ in range(2)]
    rdens = [P.sb([128, TT], F32) for _ in range(2)]
    rl = [P.sb([128, TT], BF16) for _ in range(2)]

    def load_tile(t):
        par = t % 2
        x32, yb = x32s[par], ybs[par]
        P.op('sync', lambda e: e.dma_start(out=x32[:], in_=wview(xin, None)[:, :, t * TT:(t + 1) * TT]),
             writes=[('x32', par, a) for a in range(8)], dma=('x32', par))
        P.op('sync', lambda e: e.dma_start(out=yb[:], in_=wview(yin, None)[:, :, t * TT:(t + 1) * TT]),
             writes=[('yb', par)], dma=('yb', par))

    def evac_res_for(t):
        par = t % 2
        x32 = x32s[par]

        def evac_res(f, ps, pk):
            P.op('vector', lambda e: e.tensor_tensor(out=x32[:, f, :], in0=ps[:], in1=x32[:, f, :], op=ALU.add),
                 reads=[pk, ('x32', par, f)], writes=[('x32', par, f)])
        return evac_res

    def xk_for(t):
        par = t % 2
        return lambda a: ('x32', par, a)

    def outproj(t):
        par = t % 2
        yb = ybs[par]
        self.linear(wo_mix, 1, 2, lambda a: yb[:, a, :], lambda a: [('yb', par)], evac_res_for(t))

    def norm1(t):
        self.rmsnorm(x32s[t % 2], xk_for(t), f'xa_g{l}', hb, lambda a: ('hb', a), self.sqb)

    def xattn(t):
        def evac_q(f, ps, pk):
            P.op('scalar', lambda e: e.activation(out=qb[:, f, :], in_=ps[:], func=AF.Copy, scale=1.0 / 16.0),
                 reads=[pk], writes=[('qb', f)])
        self.linear(wq, 1, 2, lambda a: hb[:, a, :], lambda a: [('hb', a)], evac_q)
        if t + 1 < ntiles:
            load_tile(t + 1)
        sc = {}

        def S_(h):
            sc[h] = []
            for mc in range(2):
                ps, pk = self.next_ps()
                for half in range(2):
                    c = 2 * h + half
                    P.op('tensor', lambda e, ps=ps, c=c, mc=mc, half=half: e.matmul(ps[:], lhsT=memK[:, c, mc * 128:(mc + 1) * 128], rhs=qb[:, c, :],
                                                                                   start=(half == 0), stop=(half == 1)),
                         reads=['memK', ('qb', c)], writes=[pk])
                sc[h].append((ps, pk))

        def E_(h):
            pT = pTs[h % 2]
            for mc in range(2):
                ps, pk = sc[h][mc]
                P.op('scalar', lambda e, ps=ps, mc=mc, pT=pT: e.activation(out=pT[:, mc, :], in_=ps[:], func=AF.Exp),
                     reads=[pk], writes=[('pT', h % 2, mc)])

        def D_(h):
            pT = pTs[h % 2]
            rden = rdens[h % 2]
            psd, pkd = self.next_ps()
            for mc in range(2):
                P.op('tensor', lambda e, psd=psd, mc=mc, pT=pT: e.matmul(psd[:], lhsT=self.onesb[:], rhs=pT[:, mc, :], start=(mc == 0), stop=(mc == 1)),
                     reads=['onesb', ('pT', h % 2, mc)], writes=[pkd])
            P.op('scalar', lambda e, psd=psd, rden=rden: e.activation(out=rden[:], in_=psd[:], func=AF.Ln), reads=[pkd], writes=[('rden', h % 2)])
            P.op('scalar', lambda e, rden=rden: e.activation(out=rden[:], in_=rden[:], func=AF.Exp, scale=-1.0), reads=[('rden', h % 2)], writes=[('rden', h % 2)])
            for half in range(2):
                c = 2 * h + half
                ps, pk = self.next_ps()
                for mc in range(2):
                    P.op('tensor', lambda e, ps=ps, c=c, mc=mc, pT=pT: e.matmul(ps[:], lhsT=memV[:, mc, c * 128:(c + 1) * 128], rhs=pT[:, mc, :],
                                                                               start=(mc == 0), stop=(mc == 1)),
                         reads=['memV', ('pT', h % 2, mc)], writes=[pk])
                P.op('vector', lambda e, ps=ps, c=c, rden=rden: e.tensor_tensor(out=ob[:, c, :], in0=ps[:], in1=rden[:], op=ALU.mult),
                     reads=[pk, ('rden', h % 2)], writes=[('ob', c)])
        S_(0)
        for h in range(4):
            if h + 1 < 4:
                S_(h + 1)
            E_(h)
            D_(h)
        self.linear(wo, 1, 2, lambda a: ob[:, a, :], lambda a: [('ob', a)], evac_res_for(t))

    def norm2(t):
        self.rmsnorm(x32s[t % 2], xk_for(t), f'mlp_g{l}', hb, lambda a: ('hb', a), self.sqb)

    def mlp1(t):
        def evac_h1(f, ps, pk):
            r = rl[f % 2]
            P.op('scalar', lambda e: e.activation(out=r[:], in_=ps[:], func=AF.Relu), reads=[pk], writes=[('rl', f % 2)])
            P.op('gpsimd', lambda e: e.tensor_tensor(out=h1[:, f, :], in0=r[:], in1=r[:], op=ALU.mult),
                 reads=[('rl', f % 2)], writes=[('h1', f)])
        self.linear(w1, 1, 8, lambda a: hb[:, a, :], lambda a: [('hb', a)], evac_h1)

    def mlp2(t):
        par = t % 2
        x32 = x32s[par]
        self.linear(w2, 4, 2, lambda a: h1[:, a, :], lambda a: [('h1', a)], evac_res_for(t))
        if final:
            self.rmsnorm(x32, xk_for(t), 'fin_g', None, xk_for(t), self.sqb, out32=x32)
        P.op('gpsimd', lambda e: e.dma_start(out=wview(xout, None)[:, :, t * TT:(t + 1) * TT], in_=x32[:]),
             reads=[('x32', par, a) for a in range(8)], writes=[x_out], dma=('xst', par), group=True)

    load_tile(0)
    outproj(0)
    norm1(0)
    for t in range(ntiles):
        nxt = t + 1 < ntiles
        xattn(t)
        norm2(t)
        if nxt:
            outproj(t + 1)
        mlp1(t)
        if nxt:
            norm1(t + 1)
        mlp2(t)
    return P.end()


Builder.dense_phase = _dense_phase


NA_CLS_M = [0, 1, 2, 62, 63]


def _na_geometry():
    p = np.arange(128)
    kr, kc = p // 64, p % 64
    j = np.arange(128)
    qr, qc = j // 64, j % 64
    valid = np.zeros((5, 128, 5, 128), bool)
    rbi = np.zeros((5, 128, 5, 128), np.int64)
    cbi = np.zeros((5, 128, 5, 128), np.int64)
    for ci, m in enumerate(NA_CLS_M):
        b = min(max(2 * m - 4, 0), 118)
        for cc in range(5):
            keyrow = b + 2 * cc + kr
            i = 2 * m + qr
            r0 = np.clip(i - 4, 0, 120)
            rv = (keyrow[:, None] >= r0[None, :]) & (keyrow[:, None] < r0[None, :] + 8)
            qcs = np.clip(qc - 8, 0, 48)
            cvd = (kc[:, None] >= qcs[None, :]) & (kc[:, None] < qcs[None, :] + 16)
            valid[ci, :, cc, :] = rv & cvd
            rbi[ci, :, cc, :] = np.clip(keyrow[:, None] - i[None, :] + 7, 0, 14)
            cbi[ci, :, cc, :] = np.clip(kc[:, None] - qc[None, :], -15, 15) + 15
    return valid, rbi, cbi


def make_consts(inp):
    c = {}
    half = 8
    inv = (500000.0 ** (-np.arange(half) * 2.0 / 16)).astype(np.float32)
    ang = np.arange(S, dtype=np.float32)[:, None] * inv[None, :]
    cos = np.cos(ang).astype(np.float32).T
    sin = np.sin(ang).astype(np.float32).T
    C = np.ones((128, S), np.float32)
    Sn = np.zeros((128, S), np.float32)
    Rm = np.zeros((128, 128), np.float32)
    for hh in range(2):
        for dd in range(16):
            C[hh * 64 + dd] = cos[dd % 8]
            Sn[hh * 64 + dd] = sin[dd % 8]
        for f in range(8):
            Rm[hh * 64 + f + 8, hh * 64 + f] = -1.0
            Rm[hh * 64 + f, hh * 64 + f + 8] = 1.0
    c['ropeC'] = C
    c['ropeS'] = Sn
    c['ropeR'] = Rm
    p = np.arange(128)[:, None]
    j = np.arange(128)[None, :]
    m0 = (p >= j)
    m1 = (p <= j)
    mA = np.zeros((128, 3, 2, 2, 128), np.float32)
    for var in range(3):
        a0 = m0 & (p >= 64) if var == 0 else m0
        a1 = m1 & (p < 64) if var == 2 else m1
        for hh in range(2):
            mA[:, var, hh, 0, :] = a0
            mA[:, var, hh, 1, :] = a1
    c['maskA'] = mA.reshape(128, 3, 512)
    valid, rbi, cbi = _na_geometry()
    c['maskNA'] = np.ascontiguousarray(valid.transpose(1, 0, 2, 3).reshape(128, 5, 640)).astype(np.float32)
    rpb = np.asarray(inp['na_rpb'][0], np.float32)
    eb = rpb[:, rbi, cbi]
    c['ebraw'] = np.ascontiguousarray(eb.transpose(2, 0, 1, 3, 4).reshape(128, 8, 5, 640))
    t = np.arange(S)
    c['rst'] = np.stack([np.broadcast_to((t % 64 != 0).astype(np.float32), (128, S)),
                         np.broadcast_to((t % 64 != 63).astype(np.float32), (128, S))], axis=1).copy()
    s_ = np.arange(64)[:, None]
    t_ = np.arange(64)[None, :]
    c['trim'] = np.stack([(s_ <= t_), (s_ >= t_)], axis=1).astype(np.float32)
    c['ident'] = np.eye(128, dtype=np.float32)
    return c


CONST_SHAPES = {'ropeC': [128, S], 'ropeS': [128, S], 'ropeR': [128, 128], 'maskA': [128, 3, 512],
                'maskNA': [128, 5, 640], 'ebraw': [128, 8, 5, 640], 'rst': [128, 2, S], 'trim': [64, 2, 64], 'ident': [128, 128]}


def _const(self, name):
    return self.dram(name, CONST_SHAPES[name], F32, kind="ExternalInput")


Builder.const = _const


def _qkv0_phase(self, x_in, ntiles=NT):
    P = self.P
    self.phase_common()
    self.norm_bufs()
    w = self.dr['ev_w_in_b0']
    xin = self.dram(x_in, [D, S], F32)
    qk_d = [self.dram(n, [512, S], BF16) for n in ['qA', 'kA', 'qB', 'kB']]
    v_d = self.dram('v_tok', [S, D], BF16)
    ropeC, ropeS = self.const('ropeC'), self.const('ropeS')
    Rm = P.sb([128, 128], BF16)
    P.op('gpsimd', lambda e: e.dma_start(out=Rm[:], in_=self.const('ropeR')), writes=['Rm'], dma='Rm')
    x32s = [P.sb([128, 8, TT], F32) for _ in range(2)]
    cs = [P.sb([128, 2, TT], F32) for _ in range(2)]
    hbs = [P.sb([128, 8, TT], BF16) for _ in range(2)]
    qkst = [P.sb([128, 16, TT], BF16) for _ in range(2)]
    vst = [P.sb([128, 4, D], BF16) for _ in range(2)]
    qs = [P.sb([128, TT], BF16) for _ in range(2)]
    t1 = [P.sb([128, TT], F32) for _ in range(2)]
    t2 = [P.sb([128, TT], F32) for _ in range(2)]

    def load_tile(t):
        par = t % 2
        P.op('sync', lambda e: e.dma_start(out=x32s[par][:], in_=wview(xin, None)[:, :, t * TT:(t + 1) * TT]),
             writes=[('x32', par)], dma=('x32', par))
        P.op('sync', lambda e: e.dma_start(out=cs[par][:, 0, :], in_=ropeC[:, t * TT:(t + 1) * TT]), writes=[('cs', par)], dma=('cs', par))
        P.op('sync', lambda e: e.dma_start(out=cs[par][:, 1, :], in_=ropeS[:, t * TT:(t + 1) * TT]), writes=[('cs', par)], dma=('cs', par), group=True)

    load_tile(0)
    cnt = [0]
    for t in range(ntiles):
        par = t % 2
        x32 = x32s[par]
        if t + 1 < ntiles:
            load_tile(t + 1)
        hb = hbs[par]
        if t == 0:
            self.rmsnorm(x32, lambda a: ('x32', par), 'mix_g0', hb, lambda a, par=par: ('hb', par, a), self.sqb)
        st = qkst[par]
        for gi, fb in enumerate([0, 1, 3, 4]):
            def evac(f, ps, pk, gi=gi, fb=fb, st=st, par=par):
                slot = gi * 4 + (f % 4)
                sc = 0.125 if fb in (0, 3) else 1.0
                if fb >= 3:
                    P.op('scalar', lambda e: e.activation(out=st[:, slot, :], in_=ps[:], func=AF.Copy, scale=sc),
                         reads=[pk], writes=[('qkst', par, slot)])
                    return
                i = cnt[0] % 2
                cnt[0] += 1
                P.op('scalar', lambda e: e.activation(out=qs[i][:], in_=ps[:], func=AF.Copy, scale=sc), reads=[pk], writes=[('qs', i)])
                psr, pkr = self.next_ps()
                P.op('tensor', lambda e: e.matmul(psr[:], lhsT=Rm[:], rhs=qs[i][:], start=True, stop=True), reads=['Rm', ('qs', i)], writes=[pkr])
                P.op('vector', lambda e: e.tensor_tensor(out=t1[i][:], in0=psr[:], in1=cs[par][:, 1, :], op=ALU.mult),
                     reads=[pkr, ('cs', par)], writes=[('t1', i)])
                P.op('gpsimd', lambda e: e.tensor_tensor(out=t2[i][:], in0=qs[i][:], in1=cs[par][:, 0, :], op=ALU.mult),
                     reads=[('qs', i), ('cs', par)], writes=[('t2', i)])
                P.op('vector', lambda e: e.tensor_tensor(out=st[:, slot, :], in0=t1[i][:], in1=t2[i][:], op=ALU.add),
                     reads=[('t1', i), ('t2', i)], writes=[('qkst', par, slot)])
            self.linear(w, 1, None, lambda a, hb=hb: hb[:, a, :], lambda a, par=par: [('hb', par, a)], evac, fbs=[fb])
        for gi in range(4):
            P.op('gpsimd', lambda e, gi=gi, st=st, t=t: e.dma_start(out=wview(qk_d[gi], None)[:, :, t * TT:(t + 1) * TT], in_=st[:, gi * 4:(gi + 1) * 4, :]),
                 reads=[('qkst', par, gi * 4 + k) for k in range(4)], writes=[['qA', 'kA', 'qB', 'kB'][gi]], dma=('qkst', par, gi), group=True)
        if t + 1 < ntiles:
            self.rmsnorm(x32s[1 - par], lambda a, par=par: ('x32', 1 - par), 'mix_g0', hbs[1 - par], lambda a, par=par: ('hb', 1 - par, a), self.sqb)
        vs = vst[par]
        for vi, fb in enumerate([2, 5]):
            wt, wk = self.load_w(w, 0, fb)
            for tc in range(4):
                ps, pk = self.next_ps()
                for a in range(8):
                    P.op('tensor', lambda e, ps=ps, wt=wt, a=a, tc=tc, hb=hb: e.matmul(ps[:], lhsT=hb[:, a, tc * 128:(tc + 1) * 128], rhs=wt[:, a, :],
                                                                               start=(a == 0), stop=(a == 7)),
                         reads=[wk, ('hb', par, a)], writes=[pk])
                eng = 'vector' if tc % 2 else 'scalar'
                if eng == 'vector':
                    P.op('vector', lambda e, ps=ps, tc=tc, vi=vi, vs=vs: e.tensor_copy(out=vs[:, tc, vi * 512:(vi + 1) * 512], in_=ps[:]),
                         reads=[pk], writes=[('vst', par)], group=True)
                else:
                    P.op('scalar', lambda e, ps=ps, tc=tc, vi=vi, vs=vs: e.activation(out=vs[:, tc, vi * 512:(vi + 1) * 512], in_=ps[:], func=AF.Copy),
                         reads=[pk], writes=[('vst', par)], group=True)
        P.op('gpsimd', lambda e, vs=vs, t=t: e.dma_start(out=v_d[t * TT:(t + 1) * TT, :].rearrange("(c p) f -> p c f", p=128), in_=vs[:]),
             reads=[('vst', par)], writes=['v_tok'], dma=('vst', par), group=True)
    return P.end()


Builder.qkv0_phase = _qkv0_phase


def _attA_phase(self, y_out, hps=range(4), branches=(1, 4, 16)):
    P = self.P
    self.phase_common(nring=0, nps=2)
    PAD = 1024
    casts_done = [False]
    qA, kA, v_d = self.dr['qA'], self.dr['kA'], self.dr['v_tok']
    yd = self.dram(y_out, [D, S], BF16)
    pss = [P.ps([128, 1024], F32) for _ in range(3)]
    qT = P.sb([128, S], BF16)
    kT = P.sb([128, S + 2 * PAD + 16], BF16)
    acc = P.sb([128, 2, S], F32)
    vb = [P.sb([128, 80, 128], BF16) for _ in range(2)]
    yb = P.sb([128, S], BF16)
    ex = [P.sb([128, 512], BF16) for _ in range(3)]
    pT = [P.sb([128, 512], BF16) for _ in range(3)]
    mask = P.sb([128, 3, 512], BF16)
    P.op('gpsimd', lambda e: e.dma_start(out=mask[:], in_=self.const('maskA')), writes=['mask'], dma='mask')
    P.op('gpsimd', lambda e: e.memset(kT[:, 0:PAD], 0.0), writes=['kTpad0'])
    P.op('gpsimd', lambda e: e.memset(kT[:, PAD + S:], 0.0), writes=['kTpad1'])
    for i in range(2):
        P.op('gpsimd', lambda e, i=i: e.memset(vb[i][:], 0.0), writes=[('vb', i)])
    vcnt = 0
    ucnt = 0
    for hp in hps:
        P.op('sync', lambda e, hp=hp: e.dma_start(out=qT[:], in_=qA[hp * 128:(hp + 1) * 128, :]), writes=['qT'], dma='qT')
        P.op('sync', lambda e, hp=hp: e.dma_start(out=kT[:, PAD:PAD + S], in_=kA[hp * 128:(hp + 1) * 128, :]), writes=['kT'], dma='kT')
        for r in branches:
            L = S // r
            nb = L // 128
            vi = vcnt % 2
            vcnt += 1
            vbuf = vb[vi]
            first = True
            for rho in range(r):
                cb = rho * (nb + 1)
                base = 64 * r + rho
                src = v_d[base:base + (nb - 1) * 128 * r, hp * 128:(hp + 1) * 128].rearrange("(g p r) f -> p g r f", p=128, r=r)[:, :, 0, :]
                P.op('sync', lambda e, src=src, cb=cb, nb=nb, vbuf=vbuf: e.dma_start(out=vbuf[:, cb + 1:cb + nb, :], in_=src),
                     writes=[('vb', vi)], dma=('vb', vi), group='vfill')
                src0 = v_d[rho:rho + 64 * r:r, hp * 128:(hp + 1) * 128]
                P.op('sync', lambda e, src0=src0, cb=cb, vbuf=vbuf: e.dma_start(out=vbuf[64:128, cb, :], in_=src0),
                     writes=[('vb', vi)], dma=('vb', vi), group='vfill')
                b2 = S - 64 * r + rho
                src1 = v_d[b2:S:r, hp * 128:(hp + 1) * 128]
                P.op('sync', lambda e, src1=src1, cb=cb, nb=nb, vbuf=vbuf: e.dma_start(out=vbuf[0:64, cb + nb, :], in_=src1),
                     writes=[('vb', vi)], dma=('vb', vi), group='vfill')
            units = [(rho, qb) for rho in range(r) for qb in range(nb)]
            if not casts_done[0]:
                casts_done[0] = True
                self.cast_queue([('ev_w_out', 0), ('xa_wq', 0), ('xa_wo', 0), ('mlp_w1', 0), ('mlp_w2', 0)])

            def emit_S(u, idx):
                rho, qb = u
                ps = pss[idx % 3]
                pk = ('pss', idx % 3)
                q0 = qb * 128 * r + rho
                for hh in range(2):
                    for c in range(2):
                        g = qb + c
                        k0 = PAD + (g * 128 - 64) * r + rho
                        P.op('tensor', lambda e, ps=ps, hh=hh, c=c, k0=k0, q0=q0, r=r: e.matmul(
                            ps[:, hh * 512 + c * 128: hh * 512 + (c + 1) * 128],
                            lhsT=kT[hh * 64:(hh + 1) * 64, k0:k0 + 127 * r + 1:r],
                            rhs=qT[hh * 64:(hh + 1) * 64, q0:q0 + 127 * r + 1:r], start=True, stop=True),
                            reads=['kT', 'qT', 'kTpad0', 'kTpad1'], writes=[pk])

            def emit_rest(u, idx):
                rho, qb = u
                i = idx % 3
                ps = pss[i]
                pk = ('pss', i)
                var = 0 if qb == 0 else (2 if qb == nb - 1 else 1)
                psv = ps[:].rearrange("p (h x) -> p h x", h=2)[:, :, 0:256]
                P.op('scalar', lambda e: e.activation(out=ex[i][:].rearrange("p (h x) -> p h x", h=2), in_=psv, func=AF.Exp),
                     reads=[pk], writes=[('ex', i)])
                P.op('vector', lambda e: e.tensor_tensor(out=pT[i][:], in0=ex[i][:], in1=mask[:, var, :], op=ALU.mult),
                     reads=[('ex', i), 'mask'], writes=[('pT', i)])
                return i

            def emit_N(u, idx):
                rho, qb = u
                i = idx % 3
                ps2, pk2 = self.next_ps()
                cb = rho * (nb + 1)
                for hh in range(2):
                    for c in range(2):
                        P.op('tensor', lambda e, hh=hh, c=c, vbuf=vbuf: e.matmul(ps2[hh * 64:(hh + 1) * 64, 0:128], lhsT=vbuf[:, cb + qb + c, hh * 64:(hh + 1) * 64],
                                                                     rhs=pT[i][:, hh * 256 + c * 128: hh * 256 + (c + 1) * 128], start=(c == 0), stop=(c == 1)),
                             reads=[('vb', vi), ('pT', i)], writes=[pk2])
                    for c in range(2):
                        P.op('tensor', lambda e, hh=hh, c=c: e.matmul(ps2[hh * 64:(hh + 1) * 64, 128:256], lhsT=self.onesb[:, 0:64],
                                                                     rhs=pT[i][:, hh * 256 + c * 128: hh * 256 + (c + 1) * 128], start=(c == 0), stop=(c == 1)),
                             reads=['onesb', ('pT', i)], writes=[pk2])
                if idx % 12 == 0:
                    self.emit_cast(after=[pk2])
                q0 = qb * 128 * r + rho
                av = acc[:, :, q0:q0 + 127 * r + 1:r]
                if r == branches[0]:
                    P.op('vector', lambda e: e.tensor_copy(out=av, in_=ps2[:, 0:256].rearrange("p (a b) -> p a b", a=2)),
                         reads=[pk2, 'acc'], writes=['acc'])
                else:
                    P.op('vector', lambda e: e.tensor_tensor(out=av, in0=ps2[:, 0:256].rearrange("p (a b) -> p a b", a=2), in1=av, op=ALU.add),
                         reads=[pk2, 'acc'], writes=['acc'])

            nu = len(units)
            for it in range(nu + 2):
                if it < nu:
                    emit_S(units[it], ucnt + it)
                if 0 <= it - 1 < nu:
                    emit_rest(units[it - 1], ucnt + it - 1)
                if 0 <= it - 2 < nu:
                    emit_N(units[it - 2], ucnt + it - 2)
            ucnt += nu
        P.op('scalar', lambda e: e.activation(out=acc[:, 1, :], in_=acc[:, 1, :], func=AF.Ln), reads=['acc'], writes=['acc'])
        P.op('scalar', lambda e: e.activation(out=acc[:, 1, :], in_=acc[:, 1, :], func=AF.Exp, scale=-1.0), reads=['acc'], writes=['acc'])
        P.op('vector', lambda e: e.tensor_tensor(out=yb[:], in0=acc[:, 0, :], in1=acc[:, 1, :], op=ALU.mult), reads=['acc'], writes=['yb'])
        P.op('sync', lambda e, hp=hp: e.dma_start(out=yd[hp * 128:(hp + 1) * 128, :], in_=yb[:]), reads=['yb'], writes=[y_out], dma='ybst', group='yst')
    self.emit_cast(n=1000)
    return P.end()


def _attB_phase(self, y_out, hps=range(4), ms=range(64)):
    P = self.P
    self.phase_common(nring=0, nps=2)
    qB, kB, v_d = self.dr['qB'], self.dr['kB'], self.dr['v_tok']
    yd = self.dram(y_out, [D, S], BF16)
    pss = [P.ps([128, 1024], F32) for _ in range(3)]
    qT = P.sb([128, S], BF16)
    kT = P.sb([128, S], BF16)
    vB = P.sb([128, 64, 128], BF16)
    yb = P.sb([128, S], BF16)
    EB = P.sb([128, 2, 5, 640], BF16)
    mNA = P.sb([128, 5, 640], F32)
    stg = [P.sb([128, 640], F32) for _ in range(2)]
    ex = [P.sb([128, 640], BF16) for _ in range(3)]
    pT = [P.sb([128, 640], BF16) for _ in range(3)]
    rden = P.sb([128, 128], F32)
    ebraw = self.const('ebraw')
    P.op('sync', lambda e: e.dma_start(out=mNA[:], in_=self.const('maskNA')), writes=['mNA'], dma='mNA')
    scnt = 0
    ucnt = 0
    for hp in hps:
        P.op('sync', lambda e, hp=hp: e.dma_start(out=qT[:], in_=qB[hp * 128:(hp + 1) * 128, :]), writes=['qT'], dma='qT')
        P.op('sync', lambda e, hp=hp: e.dma_start(out=kT[:], in_=kB[hp * 128:(hp + 1) * 128, :]), writes=['kT'], dma='kT')
        P.op('sync', lambda e, hp=hp: e.dma_start(out=vB[:], in_=v_d[:, 512 + hp * 128:512 + (hp + 1) * 128].rearrange("(g p) f -> p g f", p=128)),
             writes=['vB'], dma='vB')
        for hh in range(2):
            for cls in range(5):
                si = scnt % 2
                scnt += 1
                P.op('sync', lambda e, si=si, hh=hh, cls=cls, hp=hp: e.dma_start(out=stg[si][:], in_=ebraw[:, hp * 2 + hh, cls, :]),
                     writes=[('stg', si)], dma=('stg', si))
                P.op('scalar', lambda e, si=si: e.activation(out=stg[si][:], in_=stg[si][:], func=AF.Exp), reads=[('stg', si)], writes=[('stg', si)])
                P.op('vector', lambda e, si=si, hh=hh, cls=cls: e.tensor_tensor(out=EB[:, hh, cls, :], in0=stg[si][:], in1=mNA[:, cls, :], op=ALU.mult),
                     reads=[('stg', si), 'mNA'], writes=['EB'], group='ebfill')
        units = [(m, hh) for m in ms for hh in range(2)]
        if hp == list(hps)[0]:
            self.cast_queue([('od_w_in', 0), ('od_w_out', 0), ('xa_wq', 1), ('xa_wo', 1), ('mlp_w1', 1), ('mlp_w2', 1)])

        def geom(m):
            b = min(max(2 * m - 4, 0), 118)
            cls = 0 if m == 0 else (1 if m == 1 else (3 if m == 62 else (4 if m == 63 else 2)))
            return b // 2, cls

        def emit_S(u, idx):
            m, hh = u
            g0, cls = geom(m)
            ps = pss[idx % 3]
            pk = ('pss', idx % 3)
            for cc in range(5):
                P.op('tensor', lambda e, cc=cc: e.matmul(ps[:, cc * 128:(cc + 1) * 128], lhsT=kT[hh * 64:(hh + 1) * 64, (g0 + cc) * 128:(g0 + cc + 1) * 128],
                                                        rhs=qT[hh * 64:(hh + 1) * 64, m * 128:(m + 1) * 128], start=True, stop=True),
                     reads=['kT', 'qT'], writes=[pk])

        def emit_rest(u, idx):
            m, hh = u
            g0, cls = geom(m)
            i = idx % 3
            P.op('scalar', lambda e: e.activation(out=ex[i][:], in_=pss[i][:, 0:640], func=AF.Exp), reads=[('pss', i)], writes=[('ex', i)])
            P.op('vector', lambda e: e.tensor_tensor(out=pT[i][:], in0=ex[i][:], in1=EB[:, hh, cls, :], op=ALU.mult),
                 reads=[('ex', i), 'EB'], writes=[('pT', i)])

        ps2h = {}

        def emit_N(u, idx):
            m, hh = u
            g0, cls = geom(m)
            i = idx % 3
            if hh == 0:
                ps2h[m] = self.next_ps()
            ps2, pk2 = ps2h[m]
            if idx % 7 == 0:
                self.emit_cast(after=[('pT', i)])
            for cc in range(5):
                P.op('tensor', lambda e, cc=cc: e.matmul(ps2[hh * 64:(hh + 1) * 64, 0:128], lhsT=vB[:, g0 + cc, hh * 64:(hh + 1) * 64],
                                                        rhs=pT[i][:, cc * 128:(cc + 1) * 128], start=(cc == 0), stop=(cc == 4)),
                     reads=['vB', ('pT', i)], writes=[pk2])
            for cc in range(5):
                P.op('tensor', lambda e, cc=cc: e.matmul(ps2[hh * 64:(hh + 1) * 64, 128:256], lhsT=self.onesb[:, 0:64],
                                                        rhs=pT[i][:, cc * 128:(cc + 1) * 128], start=(cc == 0), stop=(cc == 4)),
                     reads=['onesb', ('pT', i)], writes=[pk2])
            if hh == 1:
                def fin(m=m, ps2=ps2, pk2=pk2):
                    P.op('scalar', lambda e: e.activation(out=rden[:], in_=ps2[:, 128:256], func=AF.Ln), reads=[pk2], writes=['rden'])
                    P.op('scalar', lambda e: e.activation(out=rden[:], in_=rden[:], func=AF.Exp, scale=-1.0), reads=['rden'], writes=['rden'])
                    P.op('vector', lambda e: e.tensor_tensor(out=yb[:, m * 128:(m + 1) * 128], in0=ps2[:, 0:128], in1=rden[:], op=ALU.mult),
                         reads=[pk2, 'rden'], writes=['yb'], group='ybfill')
                pendF.append(fin)

        nu = len(units)
        pendF = []
        for it in range(nu + 2):
            if it < nu:
                emit_S(units[it], ucnt + it)
            if 0 <= it - 1 < nu:
                emit_rest(units[it - 1], ucnt + it - 1)
            old_pend, pendF = pendF, []
            for f_ in old_pend:
                f_()
            if 0 <= it - 2 < nu:
                emit_N(units[it - 2], ucnt + it - 2)
        for f_ in pendF:
            f_()
        pendF = []
        ucnt += nu
        P.op('sync', lambda e, hp=hp: e.dma_start(out=yd[512 + hp * 128:512 + (hp + 1) * 128, :], in_=yb[:]), reads=['yb'], writes=[y_out], dma='ybst', group='yst')
    self.emit_cast(n=1000)
    return P.end()


Builder.attA_phase = _attA_phase
Builder.attB_phase = _attB_phase


def _lb_consts(self):
    P = self.P
    o, _ = CV['lb_logit']
    lb = P.sb([128, 8], F32)
    oml = P.sb([128, 8], F32)
    P.op('vector', lambda e: e.tensor_tensor(out=lb[:], in0=self.cv[:, o + 8:o + 16], in1=self.cv[:, o:o + 8], op=ALU.subtract),
         reads=['cv'], writes=['lb'])
    P.op('scalar', lambda e: e.activation(out=lb[:], in_=lb[:], func=AF.Sigmoid), reads=['lb'], writes=['lb'])
    P.op('vector', lambda e: e.tensor_scalar(out=oml[:], in0=lb[:], scalar1=-1.0, scalar2=1.0, op0=ALU.mult, op1=ALU.add),
         reads=['lb'], writes=['oml'])
    return lb, oml


def _proj1_phase(self, x_in, ntiles=NT):
    P = self.P
    self.phase_common()
    self.norm_bufs()
    w = self.dr['od_w_in_b0']
    xin = self.dram(x_in, [D, S], F32)
    u_d = self.dram('u1', [512, S], F32)
    gg_d = self.dram('gg1', [512, S], BF16)
    qs_d = self.dram('qs1', [512, S], BF16)
    ff_d = [self.dram(f'ff1_{d_}', [512, S], F32) for d_ in range(2)]
    gs_d = self.dram('gs1', [512, S], BF16)
    v_d = self.dram('v1_tok', [S, 512], BF16)
    lb, oml = self.lb_consts()
    x32s = [P.sb([128, 8, TT], F32) for _ in range(2)]
    hbs = [P.sb([128, 8, TT], BF16) for _ in range(2)]
    ust = [P.sb([128, 4, TT], F32) for _ in range(2)]
    fst = [[P.sb([128, 4, TT], F32) for _ in range(2)] for _ in range(2)]
    bst = [P.sb([128, 3, 4, TT], BF16) for _ in range(2)]
    vst = [P.sb([128, 4, 512], BF16) for _ in range(2)]
    sg = [P.sb([128, TT], F32) for _ in range(2)]

    def load_tile(t):
        par = t % 2
        P.op('sync', lambda e: e.dma_start(out=x32s[par][:], in_=wview(xin, None)[:, :, t * TT:(t + 1) * TT]),
             writes=[('x32', par)], dma=('x32', par))

    load_tile(0)
    cnt = [0]
    for t in range(ntiles):
        par = t % 2
        x32 = x32s[par]
        if t + 1 < ntiles:
            load_tile(t + 1)
        hb = hbs[par]
        if t == 0:
            self.rmsnorm(x32, lambda a: ('x32', par), 'mix_g1', hb, lambda a, par=par: ('hb', par, a), self.sqb)

        def evac(f, ps, pk, par=par):
            fb, c = f // 4, f % 4
            if fb == 0:
                P.op('vector', lambda e: e.tensor_copy(out=ust[par][:, c, :], in_=ps[:]), reads=[pk], writes=[('ust', par)], group='fill')
            elif fb in (1, 2, 6):
                k = {1: 0, 2: 1, 6: 2}[fb]
                fn = AF.Gelu_apprx_tanh if fb == 1 else AF.Silu
                P.op('scalar', lambda e: e.activation(out=bst[par][:, k, c, :], in_=ps[:], func=fn), reads=[pk], writes=[('bst', par, k)], group='fill')
            else:
                d_ = fb - 3
                i = cnt[0] % 2
                cnt[0] += 1
                P.op('scalar', lambda e: e.activation(out=sg[i][:], in_=ps[:], func=AF.Sigmoid), reads=[pk], writes=[('sg', i)])
                col = d_ * 4 + c
                P.op('vector', lambda e: e.tensor_scalar(out=fst[par][d_][:, c, :], in0=sg[i][:], scalar1=oml[:, col:col + 1], scalar2=lb[:, col:col + 1],
                                                         op0=ALU.mult, op1=ALU.add),
                     reads=[('sg', i), 'lb', 'oml'], writes=[('fst', par, d_)], group='fill')
        self.linear(w, 1, None, lambda a, hb=hb: hb[:, a, :], lambda a, par=par: [('hb', par, a)], evac, fbs=[0, 1, 2, 3, 4, 6])
        if t + 1 < ntiles:
            self.rmsnorm(x32s[1 - par], lambda a, par=par: ('x32', 1 - par), 'mix_g1', hbs[1 - par], lambda a, par=par: ('hb', 1 - par, a), self.sqb)
        wt, wk = self.load_w(w, 0, 5)
        for tc in range(4):
            ps, pk = self.next_ps()
            for a in range(8):
                P.op('tensor', lambda e, ps=ps, wt=wt, a=a, tc=tc, hb=hb: e.matmul(ps[:], lhsT=hb[:, a, tc * 128:(tc + 1) * 128], rhs=wt[:, a, :],
                                                                           start=(a == 0), stop=(a == 7)),
                     reads=[wk, ('hb', par, a)], writes=[pk])
            P.op('vector', lambda e, ps=ps, tc=tc, par=par: e.tensor_copy(out=vst[par][:, tc, :], in_=ps[:]), reads=[pk], writes=[('vst', par)], group='fill')
        sl = slice(t * TT, (t + 1) * TT)
        P.op('gpsimd', lambda e, par=par, sl=sl: e.dma_start(out=wview(u_d, None)[:, :, sl], in_=ust[par][:]), reads=[('ust', par)], writes=['u1'],
             dma=('ust', par), group='st')
        for k, dd in enumerate([gg_d, qs_d, gs_d]):
            P.op('gpsimd', lambda e, par=par, sl=sl, k=k, dd=dd: e.dma_start(out=wview(dd, None)[:, :, sl], in_=bst[par][:, k, :, :]),
                 reads=[('bst', par, k)], writes=[('bd', k)], dma=('bst', par, k), group='st')
        for d_ in range(2):
            P.op('gpsimd', lambda e, par=par, sl=sl, d_=d_: e.dma_start(out=wview(ff_d[d_], None)[:, :, sl], in_=fst[par][d_][:]),
                 reads=[('fst', par, d_)], writes=[('ffd', d_)], dma=('fst', par, d_), group='st')
        P.op('gpsimd', lambda e, par=par, t=t: e.dma_start(out=v_d[t * TT:(t + 1) * TT, :].rearrange("(c p) f -> p c f", p=128), in_=vst[par][:]),
             reads=[('vst', par)], writes=['v1_tok'], dma=('vst', par), group='st')
    return P.end()


Builder.lb_consts = _lb_consts
Builder.proj1_phase = _proj1_phase


def _lru_phase(self, y_out, chunks=range(4), stage=9):
    P = self.P
    self.phase_common(nring=0)
    u_d, gg_d = self.dr['u1'], self.dr['gg1']
    yd = self.dram(y_out, [D, S], BF16)
    wa_d = self.dram('lru_wa', [1, 2, 8, 64, 64], F32, kind="ExternalInput")
    wx_d = self.dram('lru_wx', [1, 2, 8, 64, 64], F32, kind="ExternalInput")
    o_lam, o_ba, o_bx, o_cw, o_cb = CV['lru_lam'][0], CV['lru_ba'][0], CV['lru_bx'][0], CV['conv_w'][0], CV['conv_b'][0]
    cv = self.cv
    oneb = P.sb([128, 1], F32)
    P.op('vector', lambda e: e.memset(oneb[:], 1.0), writes=['oneb'])
    c1 = P.sb([128, 8], F32)
    c2 = P.sb([128, 8], F32)
    P.op('scalar', lambda e: e.activation(out=c1[:], in_=cv[:, o_lam:o_lam + 8], func=AF.Exp, scale=-1.0), reads=['cv'], writes=['c1'])
    P.op('scalar', lambda e: e.activation(out=c1[:], in_=c1[:], func=AF.Ln, bias=oneb[:, 0:1]), reads=['c1', 'oneb'], writes=['c1'])
    P.op('vector', lambda e: e.tensor_scalar(out=c2[:], in0=c1[:], scalar1=-16.0, scalar2=None, op0=ALU.mult), reads=['c1'], writes=['c2'])
    P.op('vector', lambda e: e.tensor_scalar(out=c1[:], in0=c1[:], scalar1=-8.0, scalar2=None, op0=ALU.mult), reads=['c1', 'c2'], writes=['c1'])
    Wg = P.sb([128, 16, 128], BF16)
    W32 = [P.sb([128, 128], F32) for _ in range(2)]
    for i in range(2):
        P.op('vector', lambda e, i=i: e.memset(W32[i][:], 0.0), writes=[('W32', i)])
    n = 0
    for d_ in range(2):
        for gi, wd in enumerate([wa_d, wx_d]):
            for c in range(4):
                i = n % 2
                n += 1
                P.op('sync', lambda e, i=i, wd=wd, d_=d_, c=c: e.dma_start(out=W32[i][0:64, 0:64], in_=wd[0, d_, 2 * c]),
                     writes=[('W32', i)], dma=('W32', i))
                P.op('sync', lambda e, i=i, wd=wd, d_=d_, c=c: e.dma_start(out=W32[i][64:128, 64:128], in_=wd[0, d_, 2 * c + 1]),
                     writes=[('W32', i)], dma=('W32', i), group='w')
                P.op('scalar', lambda e, i=i, idx=(d_ * 2 + gi) * 4 + c: e.activation(out=Wg[:, idx, :], in_=W32[i][:], func=AF.Copy),
                     reads=[('W32', i)], writes=['Wg'], group='wg')
    if stage < 2:
        return P.end()
    bufX = P.sb([128, S + 3], F32)
    uc = P.sb([128, S], F32)
    ubf = P.sb([128, S], BF16)
    ggb = P.sb([128, S], BF16)
    yb = P.sb([128, S], BF16)
    PW = 2048
    Rs = [P.sb([128, PW], F32) for _ in range(2)]
    Is = [P.sb([128, PW], F32) for _ in range(2)]
    As = [P.sb([128, PW], F32) for _ in range(2)]
    S2s = [P.sb([128, PW], F32) for _ in range(2)]
    pcnt = [0]
    Bv = P.sb([128, PW], F32)
    H2 = P.sb([128, PW], F32)
    hst = P.sb([128, 1], F32)
    P.op('gpsimd', lambda e: e.memset(bufX[:, 0:2], 0.0), writes=['bufXp'])
    P.op('gpsimd', lambda e: e.memset(bufX[:, S + 2:S + 3], 0.0), writes=['bufXp'], group='p')
    Hs = bufX[:, 2:2 + S]
    for c in chunks:
        P.op('sync', lambda e, c=c: e.dma_start(out=bufX[:, 2:2 + S], in_=u_d[c * 128:(c + 1) * 128, :]), writes=['bufX'], dma='bufX')
        P.op('sync', lambda e, c=c: e.dma_start(out=ggb[:], in_=gg_d[c * 128:(c + 1) * 128, :]), writes=['ggb'], dma='ggb')
        for cp in range(S // PW):
            p0 = cp * PW
            P.op('vector', lambda e, c=c, p0=p0: e.tensor_scalar(out=uc[:, p0:p0 + PW], in0=bufX[:, p0:p0 + PW], scalar1=cv[:, o_cw + c:o_cw + c + 1],
                                                                 scalar2=cv[:, o_cb + c:o_cb + c + 1], op0=ALU.mult, op1=ALU.add),
                 reads=['bufX', 'bufXp', 'cv'], writes=[('uc', cp)])
            for j in range(1, 4):
                P.op('vector', lambda e, c=c, j=j, p0=p0: e.scalar_tensor_tensor(out=uc[:, p0:p0 + PW], in0=bufX[:, p0 + j:p0 + j + PW],
                                                                                 scalar=cv[:, o_cw + j * 4 + c:o_cw + j * 4 + c + 1],
                                                                                 in1=uc[:, p0:p0 + PW], op0=ALU.mult, op1=ALU.add),
                     reads=['bufX', 'bufXp', 'cv', ('uc', cp)], writes=[('uc', cp)])
            P.op('gpsimd', lambda e, p0=p0: e.tensor_copy(out=ubf[:, p0:p0 + PW], in_=uc[:, p0:p0 + PW]), reads=[('uc', cp)], writes=[('ubf', cp)])
        if stage < 3:
            continue
        for d_ in range(2):
            col = d_ * 4 + c
            order = range(4) if d_ == 0 else range(3, -1, -1)
            for pi, pc in enumerate(order):
                c0 = pc * PW
                si = pcnt[0] % 2
                pcnt[0] += 1
                R, I, A, S2 = Rs[si], Is[si], As[si], S2s[si]
                kR, kI, kA, kS2 = ('R', si), ('I', si), ('A', si), ('S2', si)
                for q in range(PW // 512):
                    for gi, (dst, dk, ob) in enumerate([(R, kR, o_ba), (I, kI, o_bx)]):
                        ps, pk = self.next_ps()
                        P.op('tensor', lambda e, ps=ps, idx=(d_ * 2 + gi) * 4 + c, s0=c0 + q * 512: e.matmul(ps[:], lhsT=Wg[:, idx, :], rhs=ubf[:, s0:s0 + 512],
                                                                                                         start=True, stop=True),
                             reads=['Wg', ('ubf', pc)], writes=[pk])
                        P.op('scalar', lambda e, ps=ps, dst=dst, q=q, bc=ob + col: e.activation(out=dst[:, q * 512:(q + 1) * 512], in_=ps[:], func=AF.Sigmoid,
                                                                                               bias=cv[:, bc:bc + 1]),
                             reads=[pk, 'cv'], writes=[dk], group='g')
                if stage < 4:
                    continue
                P.op('scalar', lambda e, col=col, S2=S2, R=R: e.activation(out=S2[:], in_=R[:], func=AF.Exp, scale=c2[:, col:col + 1]), reads=[kR, 'c2'], writes=[kS2])
                P.op('scalar', lambda e, col=col, A=A, R=R: e.activation(out=A[:], in_=R[:], func=AF.Exp, scale=c1[:, col:col + 1]), reads=[kR, 'c1'], writes=[kA])
                P.op('scalar', lambda e, S2=S2: e.activation(out=S2[:], in_=S2[:], func=AF.Sqrt, scale=-1.0, bias=oneb[:, 0:1]), reads=[kS2, 'oneb'], writes=[kS2])
                P.op('gpsimd', lambda e, c0=c0, I=I: e.tensor_tensor(out=I[:], in0=I[:], in1=uc[:, c0:c0 + PW], op=ALU.mult), reads=[kI, ('uc', pc)], writes=[kI])
                P.op('vector', lambda e, S2=S2, I=I: e.tensor_tensor(out=Bv[:], in0=S2[:], in1=I[:], op=ALU.mult), reads=[kS2, kI], writes=['Bv'])
                if stage < 5:
                    continue
                if d_ == 0:
                    init = 0.0 if (pi == 0 or stage == 5) else bufX[:, 2 + c0 - 1:2 + c0]
                    P.op('vector', lambda e, c0=c0, init=init, A=A: e.tensor_tensor_scan(out=bufX[:, 2 + c0:2 + c0 + PW], data0=A[:], data1=Bv[:], initial=init,
                                                                                  op0=ALU.mult, op1=ALU.add), reads=[kA, 'Bv', 'bufX'], writes=['bufX'])
                elif stage >= 7:
                    if pi > 0:
                        P.op('vector', lambda e: e.tensor_copy(out=hst[:], in_=H2[:, 0:1]), reads=['H2'], writes=['hst'])
                    init = 0.0 if (pi == 0 or stage == 7) else hst[:, 0:1]
                    P.op('vector', lambda e, init=init, A=A: e.tensor_tensor_scan(out=H2[:, PW - 1::-1], data0=A[:, PW - 1::-1], data1=Bv[:, PW - 1::-1], initial=init,
                                                                           op0=ALU.mult, op1=ALU.add), reads=[kA, 'Bv', 'hst'], writes=['H2'])
                    if stage >= 9:
                        P.op('vector', lambda e, c0=c0: e.tensor_tensor(out=bufX[:, 2 + c0:2 + c0 + PW], in0=bufX[:, 2 + c0:2 + c0 + PW], in1=H2[:], op=ALU.add),
                             reads=['bufX', 'H2'], writes=['bufX'])
        P.op('vector', lambda e: e.tensor_tensor(out=yb[:], in0=Hs, in1=ggb[:], op=ALU.mult), reads=['bufX', 'ggb'], writes=['yb'])
        P.op('sync', lambda e, c=c: e.dma_start(out=yd[c * 128:(c + 1) * 128, :], in_=yb[:]), reads=['yb'], writes=[y_out], dma='ybst', group='yst')
    return P.end()


Builder.lru_phase = _lru_phase


def _hgrn_phase(self, y_out, heads=range(4), dirs=(0, 1), stage=9, dbg=''):
    P = self.P
    self.phase_common(nring=0, nps=1)
    self.norm_bufs()
    qs_d, gs_d, v_d = self.dr['qs1'], self.dr['gs1'], self.dr['v1_tok']
    ff_d = [self.dr['ff1_0'], self.dr['ff1_1']]
    yd = self.dram(y_out, [D, S], BF16)
    patt = [P.ps([128, 512], F32) for _ in range(2)]
    po = [P.ps([128, 512], F32) for _ in range(2)]
    pstt = [P.ps([128, 512], F32) for _ in range(2)]
    ptr = P.ps([128, 1024], BF16)
    PW = 512
    NSET = 4
    qs = P.sb([128, S], BF16)
    vt = P.sb([128, 64, 128], BF16)
    msk = P.sb([128, PW + 1], BF16)
    Fs = [P.sb([128, PW], F32) for _ in range(NSET)]
    LFs = [P.sb([128, PW], F32) for _ in range(NSET)]
    Gs = [P.sb([128, PW], F32) for _ in range(NSET)]
    pcn = [0]
    qe = P.sb([128, S], BF16)
    ke = P.sb([128, S], BF16)
    kd = [P.sb([128, 512], BF16) for _ in range(2)]
    kdTs = [P.sb([128, 64, 128], BF16) for _ in range(2)]
    egl = P.sb([128, 128], F32)
    O = P.sb([128, S], F32)
    attb = [P.sb([128, 512], BF16) for _ in range(2)]
    tri = P.sb([128, 2, 512], BF16)
    ident = P.sb([128, 128], BF16)
    S32 = [P.sb([128, 128], F32) for _ in range(2)]
    Sbf = [P.sb([128, 128], BF16) for _ in range(2)]
    yst = [P.sb([128, 512], BF16) for _ in range(2)]
    tmpn = P.sb([128, 512], F32)
    P.op('gpsimd', lambda e: e.dma_start(out=ident[:], in_=self.const('ident')), writes=['ident'], dma='ident')
    trim = self.const('trim')
    P.op('vector', lambda e: e.memset(tri[:], 0.0), writes=['tri'])
    for half in range(2):
        for blk in range(4):
            P.op('gpsimd', lambda e, half=half, blk=blk: e.dma_start(
                out=tri[half * 64:(half + 1) * 64, :, blk * 128 + half * 64:blk * 128 + half * 64 + 64], in_=trim),
                writes=['tri'], dma='tri', group='tri')
    for i in range(2):
        P.op('vector', lambda e, i=i: e.memset(patt[i][:], 0.0), writes=['pattz'], group='pz')
    hcnt = [0, 0]
    P.op('vector', lambda e: e.memset(msk[:], 1.0), writes=['msk'])
    P.op('vector', lambda e: e.memset(msk[:, 0:PW + 1:64], 0.0), reads=['msk'], writes=['msk'])
    for i in range(2):
        P.op('gpsimd', lambda e, i=i: e.memset(kdTs[i][:], 0.0), writes=['kdTz'], group='kz')
    og = CV['gnorm'][0]
    if stage < 2:
        return P.end()
    acnt = 0
    ocnt = 0
    scnt = 0
    kcnt = [0]
    ycnt = 0
    first_dir = dirs[0]
    for hd in heads:
        P.op('sync', lambda e, hd=hd: e.dma_start(out=qs[:], in_=qs_d[hd * 128:(hd + 1) * 128, :]), writes=['qs'], dma='qs')
        P.op('sync', lambda e, hd=hd: e.dma_start(out=vt[:], in_=v_d[:, hd * 128:(hd + 1) * 128].rearrange("(g p) f -> p g f", p=128)), writes=['vt'], dma='vt')
        if 'readvt' in dbg:
            P.op('vector', lambda e: e.tensor_copy(out=tmpn[:, 0:128], in_=vt[:, 0, :]), reads=['vt'], writes=['tmpn'])
        for d_ in dirs:
            def g1(pc, hd=hd, d_=d_):
                c0 = pc * PW
                fi = pc % NSET
                F, LF, G = Fs[fi], LFs[fi], Gs[fi]
                kF, kLF, kG = ('F', fi), ('LF', fi), ('G', fi)
                P.op('sync', lambda e: e.dma_start(out=F[:], in_=ff_d[d_][hd * 128:(hd + 1) * 128, c0:c0 + PW]), writes=[kF], dma=kF)
                P.op('scalar', lambda e: e.activation(out=LF[:], in_=F[:], func=AF.Ln), reads=[kF], writes=[kLF])
                P.op('gpsimd', lambda e: e.tensor_scalar(out=F[:], in0=F[:], scalar1=-1.0, scalar2=1.0, op0=ALU.mult, op1=ALU.add), reads=[kF, kLF], writes=[kF])
                if d_ == 0:
                    P.op('vector', lambda e: e.tensor_tensor_scan(out=G[:], data0=msk[:, 0:PW], data1=LF[:], initial=0.0, op0=ALU.mult, op1=ALU.add),
                         reads=['msk', kLF], writes=[kG])
                else:
                    P.op('vector', lambda e: e.tensor_tensor_scan(out=G[:, PW - 1::-1], data0=msk[:, PW:0:-1], data1=LF[:, PW - 1::-1], initial=0.0,
                                                                  op0=ALU.mult, op1=ALU.add), reads=['msk', kLF], writes=[kG])

            def g2(pc, hd=hd, d_=d_):
                c0 = pc * PW
                fi = pc % NSET
                F, LF, G = Fs[fi], LFs[fi], Gs[fi]
                kF, kLF, kG = ('F', fi), ('LF', fi), ('G', fi)
                P.op('scalar', lambda e: e.activation(out=LF[:], in_=G[:], func=AF.Exp), reads=[kG], writes=[kLF])
                P.op('vector', lambda e: e.tensor_tensor(out=qe[:, c0:c0 + PW], in0=qs[:, c0:c0 + PW], in1=LF[:], op=ALU.mult),
                     reads=['qs', kLF], writes=[('qe', pc // 2)], group='pf')
                e0 = 63 if d_ == 0 else 0
                nch = PW // 64
                P.op('scalar', lambda e: e.activation(out=egl[:, pc * nch:(pc + 1) * nch], in_=LF[:, e0:PW:64], func=AF.Copy), reads=[kLF], writes=[('egl', pc // 2)], group='pf')

            def g3(pc, hd=hd, d_=d_):
                c0 = pc * PW
                fi = pc % NSET
                F, LF, G = Fs[fi], LFs[fi], Gs[fi]
                kF, kLF, kG = ('F', fi), ('LF', fi), ('G', fi)
                P.op('scalar', lambda e: e.activation(out=LF[:], in_=G[:], func=AF.Exp, scale=-1.0), reads=[kG, kLF], writes=[kLF])
                P.op('gpsimd', lambda e: e.tensor_tensor(out=ke[:, c0:c0 + PW], in0=F[:], in1=LF[:], op=ALU.mult), reads=[kF, kLF], writes=[('ke', pc // 2)], group='pf')

            def g4(pc, hd=hd, d_=d_):
                c0 = pc * PW
                ki = kcnt[0] % 2
                kcnt[0] += 1
                ch0 = c0 // 64
                nch = PW // 64
                P.op('vector', lambda e: e.tensor_tensor(
                    out=kd[ki][:].rearrange("p (c t) -> p c t", t=64), in0=ke[:, c0:c0 + PW].rearrange("p (c t) -> p c t", t=64),
                    in1=egl[:, ch0:ch0 + nch].unsqueeze(2).to_broadcast([128, nch, 64]), op=ALU.mult), reads=[('ke', pc // 2), ('egl', pc // 2)], writes=[('kd', ki)])
                ph = pc % 2
                for j in range(4):
                    P.op('tensor', lambda e, j=j: e.transpose(out=ptr[:, ph * 512 + j * 128:ph * 512 + (j + 1) * 128], in_=kd[ki][:, j * 128:(j + 1) * 128], identity=ident[:]),
                         reads=[('kd', ki), 'ident'], writes=[('ptr', ph)])
                b0 = c0 // 128
                P.op('scalar', lambda e: e.activation(out=kdTs[0][0:64, b0:b0 + 4, :], in_=ptr[0:64, ph * 512:(ph + 1) * 512].rearrange("p (b d) -> p b d", d=128), func=AF.Copy),
                     reads=[('ptr', ph), 'kdTz'], writes=[('kdT', pc // 2)], group='pf')
                P.op('vector', lambda e: e.tensor_copy(out=kdTs[1][64:128, b0:b0 + 4, :], in_=ptr[64:128, ph * 512:(ph + 1) * 512].rearrange("p (b d) -> p b d", d=128)),
                     reads=[('ptr', ph), 'kdTz'], writes=[('kdT', pc // 2)], group='pf')

            npc_ = S // PW
            for st_ in range(npc_ + 3):
                if st_ < npc_:
                    g1(st_)
                if 0 <= st_ - 1 < npc_:
                    g2(st_ - 1)
                if 0 <= st_ - 2 < npc_:
                    g3(st_ - 2)
                if 0 <= st_ - 3 < npc_:
                    g4(st_ - 3)
            if stage < 5:
                continue
            P.op('vector', lambda e: e.memset(S32[1][:], 0.0), writes=[('S32', 1)])
            cnt = 0
            batches = list(range(16)) if d_ == 0 else list(range(15, -1, -1))
            for bt in batches:
                b0 = bt * 4
                ai = acnt % 2
                acnt += 1
                for j in range(4):
                    for half in range(2):
                        n = 2 * (b0 + j) + half
                        c0_ = j * 128 + half * 64
                        P.op('tensor', lambda e, ai=ai, c0_=c0_, half=half, n=n: e.matmul(patt[ai][half * 64:(half + 1) * 64, c0_:c0_ + 64],
                                                                                         lhsT=ke[:, n * 64:(n + 1) * 64], rhs=qe[:, n * 64:(n + 1) * 64], start=True, stop=True),
                             reads=[('ke', n // 16), ('qe', n // 16), 'pattz'], writes=[('patt', ai)])
                P.op('vector', lambda e, ai=ai, d_=d_: e.tensor_tensor(out=attb[ai][:], in0=patt[ai][:], in1=tri[:, d_, :], op=ALU.mult),
                     reads=[('patt', ai), 'tri'], writes=[('attb', ai)])
                if stage < 6:
                    continue
                oi = ocnt % 2
                ocnt += 1
                blocks = list(range(b0, b0 + 4))
                if d_ == 1:
                    blocks = blocks[::-1]
                pinfo = {}
                for blk in blocks:
                    for n in ([2 * blk, 2 * blk + 1] if d_ == 0 else [2 * blk + 1, 2 * blk]):
                        half = n % 2
                        sslot = hcnt[half] % 4
                        hcnt[half] += 1
                        pst_t = pstt[half]
                        pkey = 'pstb' if 'nohoist' not in dbg else ('pst', half, sslot)
                        pinfo[n] = (pst_t, sslot, pkey)
                        if 'nohoist' not in dbg:
                          P.op('tensor', lambda e, pst_t=pst_t, sslot=sslot, half=half, blk=blk: e.matmul(
                            pst_t[:, sslot * 128:(sslot + 1) * 128], lhsT=kdTs[half][:, blk, :], rhs=vt[:, blk, :],
                            start=True, stop=True), reads=[('kdT', blk // 8), 'vt'], writes=[pkey])
                for blk in blocks:
                    j = blk - b0
                    P.op('tensor', lambda e, oi=oi, j=j, blk=blk, ai=ai: e.matmul(po[oi][:, j * 128:(j + 1) * 128], lhsT=vt[:, blk, :], rhs=attb[ai][:, j * 128:(j + 1) * 128],
                                                                                 start=True, stop=False), reads=['vt', ('attb', ai)], writes=[('po', oi)])
                    chunks = [2 * blk, 2 * blk + 1] if d_ == 0 else [2 * blk + 1, 2 * blk]
                    for ci, n in enumerate(chunks):
                        half = n % 2
                        pst_t, sslot, pkey = pinfo[n]
                        cur, prev = cnt % 2, (cnt - 1) % 2
                        if 'nohoist' in dbg:
                            P.op('tensor', lambda e, pst_t=pst_t, sslot=sslot, half=half, blk=blk: e.matmul(
                                pst_t[:, sslot * 128:(sslot + 1) * 128], lhsT=kdTs[half][:, blk, :], rhs=vt[:, blk, :],
                                start=True, stop=True), reads=[('kdT', blk // 8), 'vt'], writes=[pkey])
                        if cnt > 0:
                            P.op('tensor', lambda e, oi=oi, c0_=j * 128 + half * 64, n=n, prev=prev, last=(ci == 1): e.matmul(
                                po[oi][:, c0_:c0_ + 64], lhsT=Sbf[prev][:], rhs=qe[:, n * 64:(n + 1) * 64], start=False, stop=last),
                                reads=[('Sbf', prev), ('qe', n // 16)], writes=[('po', oi)])
                        P.op('vector', lambda e, cur=cur, prev=prev, n=n, pst_t=pst_t, sslot=sslot: e.scalar_tensor_tensor(
                            out=S32[cur][:], in0=S32[prev][:], scalar=egl[:, n:n + 1], in1=pst_t[:, sslot * 128:(sslot + 1) * 128], op0=ALU.mult, op1=ALU.add),
                            reads=[('S32', prev), ('egl', n // 16), pkey], writes=[('S32', cur)])
                        P.op('scalar', lambda e, cur=cur: e.activation(out=Sbf[cur][:], in_=S32[cur][:], func=AF.Copy), reads=[('S32', cur)], writes=[('Sbf', cur)])
                        cnt += 1
                t0 = b0 * 128
                if d_ == first_dir:
                    P.op('scalar', lambda e, oi=oi, t0=t0: e.activation(out=O[:, t0:t0 + 512], in_=po[oi][:], func=AF.Copy), reads=[('po', oi)], writes=['O'])
                else:
                    P.op('vector', lambda e, oi=oi, t0=t0: e.tensor_tensor(out=O[:, t0:t0 + 512], in0=po[oi][:], in1=O[:, t0:t0 + 512], op=ALU.add),
                         reads=[('po', oi), 'O'], writes=['O'])
        if stage < 8:
            continue
        P.op('sync', lambda e, hd=hd: e.dma_start(out=qs[:], in_=gs_d[hd * 128:(hd + 1) * 128, :]), writes=['qs'], dma='qs')
        for pc in range(S // 512):
            t0 = pc * 512
            sq = self.sqb[pc % 2]
            psn, pkn = self.next_ps()
            P.op('scalar', lambda e, sq=sq, t0=t0: e.activation(out=sq[:], in_=O[:, t0:t0 + 512], func=AF.Square), reads=['O'], writes=[('sq', pc % 2)])
            P.op('tensor', lambda e, sq=sq, psn=psn: e.matmul(psn[:], lhsT=self.onesb[:], rhs=sq[:], start=True, stop=True), reads=[('sq', pc % 2), 'onesb'], writes=[pkn])
            P.op('scalar', lambda e, psn=psn: e.activation(out=self.rstd[:], in_=psn[:], func=AF.Ln, scale=1.0 / 128.0, bias=self.epsb[:, 0:1]),
                 reads=[pkn, 'epsb'], writes=['rstd'])
            P.op('scalar', lambda e: e.activation(out=self.rstd[:], in_=self.rstd[:], func=AF.Exp, scale=-0.5), reads=['rstd'], writes=['rstd'])
            P.op('vector', lambda e, t0=t0: e.scalar_tensor_tensor(out=tmpn[:], in0=O[:, t0:t0 + 512], scalar=self.cv[:, og:og + 1], in1=self.rstd[:],
                                                                   op0=ALU.mult, op1=ALU.mult), reads=['O', 'cv', 'rstd'], writes=['tmpn'])
            yi = ycnt % 2
            ycnt += 1
            P.op('gpsimd', lambda e, yi=yi, t0=t0: e.tensor_tensor(out=yst[yi][:], in0=tmpn[:], in1=qs[:, t0:t0 + 512], op=ALU.mult),
                 reads=['tmpn', 'qs'], writes=[('yst', yi)])
            P.op('sync', lambda e, yi=yi, t0=t0, hd=hd: e.dma_start(out=yd[512 + hd * 128:512 + (hd + 1) * 128, t0:t0 + 512], in_=yst[yi][:]),
                 reads=[('yst', yi)], writes=[y_out], dma=('yst', yi), group='yst')
    return P.end()


Builder.hgrn_phase = _hgrn_phase


_CACHE = {}
W_NAMES = ['ev_w_in', 'ev_w_out', 'od_w_in', 'od_w_out', 'xa_wq', 'xa_wkv', 'xa_wo', 'mlp_w1', 'mlp_w2', 'lru_wa', 'lru_wx']
C_NAMES = ['ropeC', 'ropeS', 'ropeR', 'maskA', 'maskNA', 'ebraw', 'trim', 'ident']


def build_full():
    B = Builder(ext_in=['xT'], ext_out=['outT'])
    stats = []
    stats.append(B.prep_phase())
    stats.append(B.qkv0_phase('xT'))
    stats.append(B.attA_phase('y0T'))
    stats.append(B.attB_phase('y0T'))
    stats.append(B.dense_phase(0, 'xT', 'y0T', 'xl0T', False))
    stats.append(B.proj1_phase('xl0T'))
    stats.append(B.lru_phase('y1T'))
    stats.append(B.hgrn_phase('y1T'))
    stats.append(B.dense_phase(1, 'xl0T', 'y1T', 'outT', True))
    return B, stats


def kernel(**inputs):
    inp = {k: np.asarray(v) for k, v in inputs.items()}
    if 'B' not in _CACHE:
        _CACHE['B'], _CACHE['stats'] = build_full()
    B = _CACHE['B']
    n = 8
    shared = {k: np.ascontiguousarray(inp[k], dtype=np.float32) for k in W_NAMES}
    shared['cvec'] = pack_cvec(inp)
    cs = make_consts(inp)
    for k in C_NAMES:
        shared[k] = np.ascontiguousarray(cs[k], dtype=np.float32)
    in_maps = []
    for b in range(n):
        m = dict(shared)
        m['xT'] = np.ascontiguousarray(inp['x'][b].T, dtype=np.float32)
        m['memT'] = np.ascontiguousarray(inp['mem'][b].T, dtype=np.float32)
        in_maps.append(m)
    res = run_bass_kernel_spmd(B.nc, in_maps, core_ids=list(range(n)))
    out = np.stack([np.ascontiguousarray(np.asarray(r['outT']).T) for r in res.results], axis=0)
    return out.astype(np.float32)
```

```python
import math
from contextlib import ExitStack
import numpy as np
import concourse.bass as bass
import concourse.mybir as mybir
from concourse.bass_utils import run_bass_kernel_spmd

F32 = mybir.dt.float32
BF16 = mybir.dt.bfloat16
AF = mybir.ActivationFunctionType
ALU = mybir.AluOpType

ENGS = ['sync', 'scalar', 'vector', 'gpsimd', 'tensor']
S = 8192
D = 1024
TT = 512
NT = S // TT
EPS = 1e-6


class _Op:
    __slots__ = ('eng', 'fn', 'waits', 'ordinal', 'dma')


class Prog:
    def __init__(self, nc):
        self.nc = nc
        self.esem = {e: nc.alloc_semaphore(f"cnt_{e}") for e in ENGS}
        self.base = {e: 0 for e in ENGS}
        self.dcnt = {}
        self.dsems = {}
        self.nt = 0
        self.stack = None
        self._reset()

    def _reset(self):
        self.streams = {e: [] for e in ENGS}
        self.nord = {e: 0 for e in ENGS}
        self.seen = {e: {} for e in ENGS}
        self.bufs = {}

    def begin(self):
        self.stack = ExitStack()
        self._reset()

    def sb(self, shape, dtype, name=None):
        self.nt += 1
        return self.stack.enter_context(self.nc.sbuf_tensor(name or f"t{self.nt}", list(shape), dtype))

    def ps(self, shape, dtype=F32, name=None):
        self.nt += 1
        return self.stack.enter_context(self.nc.psum_tensor(name or f"p{self.nt}", list(shape), dtype))

    def _buf(self, k):
        b = self.bufs.get(k)
        if b is None:
            b = self.bufs[k] = [{}, {}, {}, None]
        return b

    def op(self, eng, fn, reads=(), writes=(), dma=None, group=None, deps=()):
        if group is False:
            group = None
        waits = {}

        def need(d):
            for s, v in d.items():
                if v > waits.get(s, 0):
                    waits[s] = v
        for k in deps:
            need(self._buf(k)[0])
        for k in reads:
            need(self._buf(k)[0])
        joins = []
        for k in writes:
            b = self._buf(k)
            j = group is not None and b[3] == group and not b[1]
            joins.append(j)
            if j:
                need(b[2])
            else:
                need(b[0])
                need(b[1])
        seen = self.seen[eng]
        fw = {}
        for s, v in waits.items():
            if s == ('e', eng) and eng == 'tensor':
                continue
            if v > seen.get(s, 0):
                seen[s] = v
                fw[s] = v
        o = _Op()
        o.eng = eng
        o.fn = fn
        o.waits = fw
        o.dma = dma
        if fn is None:
            o.ordinal = None
            self.streams[eng].append(o)
            return o
        if dma is None:
            self.nord[eng] += 1
            o.ordinal = self.nord[eng]
            ev = (('e', eng), o.ordinal)
        else:
            if dma not in self.dsems:
                self.dsems[dma] = self.nc.alloc_semaphore(f"d{len(self.dsems)}")
                self.dcnt[dma] = 0
            self.dcnt[dma] += 16
            o.ordinal = None
            ev = (('d', dma), self.dcnt[dma])
        self.streams[eng].append(o)
        for k in reads:
            b = self._buf(k)
            if ev[1] > b[1].get(ev[0], 0):
                b[1][ev[0]] = ev[1]
        for k, j in zip(writes, joins):
            b = self._buf(k)
            if j:
                b[0][ev[0]] = max(ev[1], b[0].get(ev[0], 0))
            else:
                if group is not None:
                    pre = dict(b[0])
                    for s_, v_ in b[1].items():
                        if v_ > pre.get(s_, 0):
                            pre[s_] = v_
                    b[2] = pre
                else:
                    b[2] = {}
                b[0] = {ev[0]: ev[1]}
                b[1] = {}
                b[3] = group
        return o

    def _simulate(self, rank):
        val = {}
        for e in ENGS:
            val[('e', e)] = self.base[e]
        pos = {e: 0 for e in ENGS}
        dval = getattr(self, '_dval', {})
        progress = True
        while progress:
            progress = False
            for e in ENGS:
                st = self.streams[e]
                while pos[e] < len(st):
                    o = st[pos[e]]
                    ok = True
                    for s_, v in o.waits.items():
                        if s_[0] == 'e':
                            if val[s_] < rank[s_[1]][v]:
                                ok = False
                                break
                        elif dval.get(s_[1], 0) < v:
                            ok = False
                            break
                    if not ok:
                        break
                    if o.fn is not None:
                        if o.dma is not None:
                            dval[o.dma] = dval.get(o.dma, 0) + 16
                        elif o.ordinal in rank[e]:
                            val[('e', e)] += 1
                    pos[e] += 1
                    progress = True
        self._dval = dval
        stuck = {e: pos[e] for e in ENGS if pos[e] < len(self.streams[e])}
        if stuck:
            raise RuntimeError(f"sync deadlock detected at build time: {stuck}")

    def end(self):
        nc = self.nc
        o = _Op()
        o.eng = 'sync'
        o.fn = None
        o.dma = None
        o.ordinal = None
        o.waits = {('d', k): v for k, v in self.dcnt.items() if v > self.seen['sync'].get(('d', k), 0)}
        self.streams['sync'].append(o)
        marked = {e: set() for e in ENGS}
        for e in ENGS:
            for op_ in self.streams[e]:
                for s, v in op_.waits.items():
                    if s[0] == 'e':
                        marked[s[1]].add(v)
        rank = {}
        for e in ENGS:
            rank[e] = {v: self.base[e] + i + 1 for i, v in enumerate(sorted(marked[e]))}
        streams = self.streams
        esem = self.esem
        dsems = self.dsems
        self._simulate(rank)

        def body_for(e):
            def body(eng):
                for op_ in streams[e]:
                    for s, v in op_.waits.items():
                        if s[0] == 'e':
                            eng.wait_ge(esem[s[1]], rank[s[1]][v])
                        else:
                            eng.wait_ge(dsems[s[1]], v)
                    if op_.fn is None:
                        continue
                    ins = op_.fn(eng)
                    if op_.dma is not None:
                        ins.then_inc(dsems[op_.dma], 16)
                    elif op_.ordinal in rank[e]:
                        ins.then_inc(esem[e], 1)
            return body
        with nc.Block() as block:
            block.sync(body_for('sync'))
            block.scalar(body_for('scalar'))
            block.vector(body_for('vector'))
            block.gpsimd(body_for('gpsimd'))
            block.tensor(body_for('tensor'))
        n = {e: len(streams[e]) for e in ENGS}
        for e in ENGS:
            self.base[e] += len(marked[e])
        self.stack.close()
        self.stack = None
        self._reset()
        return n


CV = {}


def _cv_layout():
    off = 0
    for name, n in [('mix_g0', 8), ('xa_g0', 8), ('mem_g0', 8), ('mlp_g0', 8),
                    ('mix_g1', 8), ('xa_g1', 8), ('mem_g1', 8), ('mlp_g1', 8), ('fin_g', 8),
                    ('conv_w', 16), ('conv_b', 4), ('lru_ba', 8), ('lru_bx', 8), ('lru_lam', 8),
                    ('lb_logit', 16), ('gnorm', 1)]:
        CV[name] = (off, n)
        off += n
    return off


NCV = _cv_layout()


def _colmajor(v):
    v = np.asarray(v, np.float32).reshape(-1, 128)
    return v.T


def pack_cvec(inp):
    cv = np.zeros((128, NCV), np.float32)

    def put(name, arr):
        o, n = CV[name]
        assert arr.shape == (128, n), (name, arr.shape)
        cv[:, o:o + n] = arr
    for l in range(2):
        put(f'mix_g{l}', _colmajor(inp['norm_mix_g'][l]))
        put(f'xa_g{l}', _colmajor(inp['norm_xa_g'][l]))
        put(f'mem_g{l}', _colmajor(inp['norm_mem_g'][l]))
        put(f'mlp_g{l}', _colmajor(inp['norm_mlp_g'][l]))
    put('fin_g', _colmajor(inp['final_norm_g']))
    put('conv_w', np.concatenate([_colmajor(inp['conv_w'][0, j]) for j in range(4)], axis=1))
    put('conv_b', _colmajor(inp['conv_b'][0]))
    put('lru_ba', np.concatenate([_colmajor(inp['lru_ba'][0, d_]) for d_ in range(2)], axis=1))
    put('lru_bx', np.concatenate([_colmajor(inp['lru_bx'][0, d_]) for d_ in range(2)], axis=1))
    put('lru_lam', np.concatenate([_colmajor(inp['lru_lambda'][0, d_]) for d_ in range(2)], axis=1))
    put('lb_logit', np.concatenate([_colmajor(inp['hgrn_lb_logits'][l, d_]) for l in range(2) for d_ in range(2)], axis=1))
    put('gnorm', np.asarray(inp['hgrn_norm_g'][0], np.float32).reshape(128, 1))
    return cv


class Builder:
    def __init__(self, ext_in=(), ext_out=()):
        self.nc = bass.Bass("TRN2", target_bir_lowering=False)
        self.P = Prog(self.nc)
        self.ext_in = set(ext_in)
        self.ext_out = set(ext_out)
        self.dr = {}
        self.psn = 0

    def dram(self, name, shape, dtype, kind=None):
        if name in self.dr:
            return self.dr[name]
        if kind is None:
            kind = "ExternalInput" if name in self.ext_in else ("ExternalOutput" if name in self.ext_out else "Internal")
        t = self.nc.dram_tensor(name, list(shape), dtype, kind=kind).ap()
        self.dr[name] = t
        return t

    def cast_weight(self, src, dst, K, F, key):
        P = self.P
        fc = min(F, 512)
        for a in range(K // 128):
            s = src[a * 128:(a + 1) * 128, :].rearrange("p (c f) -> p c f", f=fc)
            d = dst[a * 128:(a + 1) * 128, :].rearrange("p (c f) -> p c f", f=fc)
            P.op('gpsimd', lambda e, s=s, d=d: e.dma_start(out=d, in_=s), writes=[key], dma=key, group=True)


def wview(w, K):
    return w.rearrange("(a p) f -> p a f", p=128)


def _phase_common(self, nring=5, nps=8):
    P = self.P
    P.begin()
    self.pst = [P.ps([128, 512], F32) for _ in range(nps)]
    self.nps = nps
    self.psi = 0
    self.wring = [P.sb([128, 8, 512], BF16) for _ in range(nring)]
    self.wri = 0
    self.cv = P.sb([128, NCV], F32)
    cvd = self.dram('cvec', [128, NCV], F32, kind="ExternalInput")
    P.op('sync', lambda e: e.dma_start(out=self.cv[:], in_=cvd), writes=['cv'], dma='cv')
    self.ones32 = P.sb([128, 128], F32)
    self.onesb = P.sb([128, 128], BF16)
    P.op('vector', lambda e: e.memset(self.ones32[:], 1.0), writes=['ones32'])
    P.op('vector', lambda e: e.memset(self.onesb[:], 1.0), writes=['onesb'])


def _next_ps(self):
    i = self.psi % self.nps
    self.psi += 1
    return self.pst[i], ('ps', i)


def _load_w(self, wd, kb, fb, nk=8, wkey=None):
    P = self.P
    i = self.wri % len(self.wring)
    self.wri += 1
    t = self.wring[i]
    src = wview(wd, None)[:, kb * 8:kb * 8 + nk, fb * 512:(fb + 1) * 512]
    P.op('sync', lambda e, t=t, src=src: e.dma_start(out=t[:, 0:nk, :], in_=src),
         reads=[wkey] if wkey else [], writes=[('wr', i)], dma=('wr', i))
    return t, ('wr', i)


def _linear(self, wd, nkb, nfb, rhs, rkeys, evac, wkey=None, T=TT, nk=8, fbs=None):
    P = self.P
    for fb in (fbs if fbs is not None else range(nfb)):
        pss = [self.next_ps() for _ in range(4)]
        for kb in range(nkb):
            wt, wk = self.load_w(wd, kb, fb, nk=nk, wkey=wkey)
            for f4 in range(4):
                ps, pk = pss[f4]
                for a in range(nk):
                    P.op('tensor', lambda e, ps=ps, wt=wt, a=a, f4=f4, ga=kb * 8 + a, st=(kb == 0 and a == 0), sp=(kb == nkb - 1 and a == nk - 1):
                         e.matmul(ps[:, 0:T], lhsT=wt[:, a, f4 * 128:(f4 + 1) * 128], rhs=rhs(ga), start=st, stop=sp),
                         reads=[wk] + rkeys(kb * 8 + a), writes=[pk])
        for f4 in range(4):
            ps, pk = pss[f4]
            evac(fb * 4 + f4, ps, pk)


def _rmsnorm(self, x32, xkeys, gname, hb, hkeys, sq, T=TT, out32=None):
    P = self.P
    go, _ = CV[gname]
    ps, pk = self.next_ps()
    for a in range(8):
        s = sq[a % 4]
        P.op('scalar', lambda e, s=s, a=a: e.activation(out=s[:, 0:T], in_=x32[:, a, 0:T], func=AF.Square),
             reads=[xkeys(a)], writes=[('sq', a % 4)])
        P.op('tensor', lambda e, s=s, a=a, ps=ps: e.matmul(ps[:, 0:T], lhsT=self.onesb[:], rhs=s[:, 0:T], start=(a == 0), stop=(a == 7)),
             reads=[('sq', a % 4), 'onesb'], writes=[pk])
    rstd = self.rstd
    P.op('scalar', lambda e, ps=ps: e.activation(out=rstd[:, 0:T], in_=ps[:, 0:T], func=AF.Ln, scale=1.0 / D, bias=self.epsb[:, 0:1]),
         reads=[pk, 'epsb'], writes=['rstd'])
    P.op('scalar', lambda e: e.activation(out=rstd[:, 0:T], in_=rstd[:, 0:T], func=AF.Exp, scale=-0.5), reads=['rstd'], writes=['rstd'])
    for a in range(8):
        dst = hb if out32 is None else out32
        P.op('vector', lambda e, a=a, dst=dst: e.scalar_tensor_tensor(out=dst[:, a, 0:T], in0=x32[:, a, 0:T], scalar=self.cv[:, go + a:go + a + 1],
                                                                      in1=rstd[:, 0:T], op0=ALU.mult, op1=ALU.mult),
             reads=[xkeys(a), 'rstd', 'cv'], writes=[hkeys(a)])


def _norm_bufs(self, T=TT):
    P = self.P
    self.sqb = [P.sb([128, T], BF16) for _ in range(4)]
    self.rstd = P.sb([128, T], F32)
    self.epsb = P.sb([128, 1], F32)
    P.op('vector', lambda e: e.memset(self.epsb[:], EPS), writes=['epsb'])


Builder.phase_common = _phase_common
Builder.next_ps = _next_ps
Builder.load_w = _load_w
Builder.linear = _linear
Builder.rmsnorm = _rmsnorm
Builder.norm_bufs = _norm_bufs


WSPECS = {'ev_w_in': (1, 1024, 3072), 'ev_w_out': (1, 1024, 1024), 'od_w_in': (1, 1024, 3584), 'od_w_out': (1, 1024, 1024),
          'xa_wq': (2, 1024, 1024), 'xa_wkv': (2, 1024, 2048), 'xa_wo': (2, 1024, 1024), 'mlp_w1': (2, 1024, 4096), 'mlp_w2': (2, 4096, 1024)}


def _cast_list(self, items, after=()):
    if after:
        self.P.op('gpsimd', None, reads=list(after))
    for name, l in items:
        L, K, F = WSPECS[name]
        src = self.dram(name, [L, K, F], F32, kind="ExternalInput")
        dst = self.dram(f'{name}_b{l}', [K, F], BF16)
        self.cast_weight(src[l], dst, K, F, key=f'{name}_b{l}')


Builder.cast_list = _cast_list


def _cast_queue(self, items):
    q = []
    for name, l in items:
        L, K, F = WSPECS[name]
        src = self.dram(name, [L, K, F], F32, kind="ExternalInput")
        dst = self.dram(f'{name}_b{l}', [K, F], BF16)
        fc = min(F, 512)
        for a in range(K // 128):
            sa = src[l][a * 128:(a + 1) * 128, :].rearrange("p (c f) -> p c f", f=fc)
            da = dst[a * 128:(a + 1) * 128, :].rearrange("p (c f) -> p c f", f=fc)
            q.append((sa, da, f'{name}_b{l}'))
    self.castq = q


def _emit_cast(self, after=(), n=1):
    for _ in range(n):
        if not getattr(self, 'castq', None):
            return
        sa, da, key = self.castq.pop(0)
        self.P.op('gpsimd', lambda e, sa=sa, da=da: e.dma_start(out=da, in_=sa), deps=list(after), writes=[key], dma=key, group=True)


Builder.cast_queue = _cast_queue
Builder.emit_cast = _emit_cast

def _prep_phase(self):
    P = self.P
    self.phase_common()
    self.norm_bufs(256)
    self.cast_list([('xa_wkv', 0), ('xa_wkv', 1), ('ev_w_in', 0)])
    memT = self.dram('memT', [D, 256], F32, kind="ExternalInput")
    m32 = P.sb([128, 8, 256], F32)
    P.op('sync', lambda e: e.dma_start(out=m32[:], in_=wview(memT, None)), writes=['m32'], dma='m32')
    hm = P.sb([128, 8, 256], BF16)
    kst = P.sb([128, 8, 256], BF16)
    vst = P.sb([128, 2, 1024], BF16)
    for l in range(2):
        self.rmsnorm(m32, lambda a: 'm32', f'mem_g{l}', hm, lambda a: 'hm', self.sqb, T=256)
        wkv = self.dr[f'xa_wkv_b{l}']
        kd = self.dram(f'memK{l}', [D, 256], BF16)
        vd = self.dram(f'memV{l}', [256, D], BF16)

        def evac_k(f, ps, pk):
            P.op('scalar', lambda e, f=f, ps=ps: e.activation(out=kst[:, f, :], in_=ps[:, 0:256], func=AF.Copy),
                 reads=[pk], writes=['kst'])
        self.linear(wkv, 1, 2, lambda a: hm[:, a, :], lambda a: ['hm'], evac_k, wkey=f'xa_wkv_b{l}', T=256)
        P.op('gpsimd', lambda e, kd=kd: e.dma_start(out=wview(kd, None), in_=kst[:]), reads=['kst'], writes=[f'memK{l}'], dma='kst')
        for fb in range(2):
            wt, wk = self.load_w(wkv, 0, 2 + fb, wkey=f'xa_wkv_b{l}')
            for mc in range(2):
                ps, pk = self.next_ps()
                for a in range(8):
                    P.op('tensor', lambda e, ps=ps, wt=wt, a=a, mc=mc: e.matmul(ps[:], lhsT=hm[:, a, mc * 128:(mc + 1) * 128], rhs=wt[:, a, :],
                                                                               start=(a == 0), stop=(a == 7)),
                         reads=[wk, 'hm'], writes=[pk])
                P.op('vector', lambda e, ps=ps, mc=mc, fb=fb: e.tensor_copy(out=vst[:, mc, fb * 512:(fb + 1) * 512], in_=ps[:]),
                     reads=[pk], writes=['vst'])
        P.op('gpsimd', lambda e, vd=vd: e.dma_start(out=vd.rearrange("(c p) f -> p c f", p=128), in_=vst[:]),
             reads=['vst'], writes=[f'memV{l}'], dma='vst')
    return P.end()


Builder.prep_phase = _prep_phase


def _dense_phase(self, layer, x_in, y_in, x_out, final, ntiles=NT):
    P = self.P
    self.phase_common(nring=6)
    self.norm_bufs()
    l = layer
    wo_mix = self.dr['ev_w_out_b0' if l == 0 else 'od_w_out_b0']
    wq = self.dr[f'xa_wq_b{l}']
    wo = self.dr[f'xa_wo_b{l}']
    w1 = self.dr[f'mlp_w1_b{l}']
    w2 = self.dr[f'mlp_w2_b{l}']
    xin = self.dram(x_in, [D, S], F32)
    yin = self.dram(y_in, [D, S], BF16)
    xout = self.dram(x_out, [D, S], F32)
    memK = P.sb([128, 8, 256], BF16)
    memV = P.sb([128, 2, 1024], BF16)
    P.op('sync', lambda e: e.dma_start(out=memK[:], in_=wview(self.dr[f'memK{l}'], None)), writes=['memK'], dma='memK')
    P.op('sync', lambda e: e.dma_start(out=memV[:], in_=self.dr[f'memV{l}'].rearrange("(c p) f -> p c f", p=128)), writes=['memV'], dma='memV')
    x32s = [P.sb([128, 8, TT], F32) for _ in range(2)]
    ybs = [P.sb([128, 8, TT], BF16) for _ in range(2)]
    hb = P.sb([128, 8, TT], BF16)
    qb = P.sb([128, 8, TT], BF16)
    ob = P.sb([128, 8, TT], BF16)
    h1 = P.sb([128, 32, TT], BF16)
    pTs = [P.sb([128, 2, TT], BF16) for _ in range(2)]
    rdens = [P.sb([128, TT], F32) for _ in range(2)]
    rl = [P.sb([128, TT], BF16) for _ in range(2)]

    def load_tile(t):
        par = t % 2
        x32, yb = x32s[par], ybs[par]
        P.op('sync', lambda e: e.dma_start(out=x32[:], in_=wview(xin, None)[:, :, t * TT:(t + 1) * TT]),
             writes=[('x32', par, a) for a in range(8)], dma=('x32', par))
        P.op('sync', lambda e: e.dma_start(out=yb[:], in_=wview(yin, None)[:, :, t * TT:(t + 1) * TT]),
             writes=[('yb', par)], dma=('yb', par))

    def evac_res_for(t):
        par = t % 2
        x32 = x32s[par]

        def evac_res(f, ps, pk):
            P.op('vector', lambda e: e.tensor_tensor(out=x32[:, f, :], in0=ps[:], in1=x32[:, f, :], op=ALU.add),
                 reads=[pk, ('x32', par, f)], writes=[('x32', par, f)])
        return evac_res

    def xk_for(t):
        par = t % 2
        return lambda a: ('x32', par, a)

    def outproj(t):
        par = t % 2
        yb = ybs[par]
        self.linear(wo_mix, 1, 2, lambda a: yb[:, a, :], lambda a: [('yb', par)], evac_res_for(t))

    def norm1(t):
        self.rmsnorm(x32s[t % 2], xk_for(t), f'xa_g{l}', hb, lambda a: ('hb', a), self.sqb)

    def xattn(t):
        def evac_q(f, ps, pk):
            P.op('scalar', lambda e: e.activation(out=qb[:, f, :], in_=ps[:], func=AF.Copy, scale=1.0 / 16.0),
                 reads=[pk], writes=[('qb', f)])
        self.linear(wq, 1, 2, lambda a: hb[:, a, :], lambda a: [('hb', a)], evac_q)
        if t + 1 < ntiles:
            load_tile(t + 1)
        sc = {}

        def S_(h):
            sc[h] = []
            for mc in range(2):
                ps, pk = self.next_ps()
                for half in range(2):
                    c = 2 * h + half
                    P.op('tensor', lambda e, ps=ps, c=c, mc=mc, half=half: e.matmul(ps[:], lhsT=memK[:, c, mc * 128:(mc + 1) * 128], rhs=qb[:, c, :],
                                                                                   start=(half == 0), stop=(half == 1)),
                         reads=['memK', ('qb', c)], writes=[pk])
                sc[h].append((ps, pk))

        def E_(h):
            pT = pTs[h % 2]
            for mc in range(2):
                ps, pk = sc[h][mc]
                P.op('scalar', lambda e, ps=ps, mc=mc, pT=pT: e.activation(out=pT[:, mc, :], in_=ps[:], func=AF.Exp),
                     reads=[pk], writes=[('pT', h % 2, mc)])

        def D_(h):
            pT = pTs[h % 2]
            rden = rdens[h % 2]
            psd, pkd = self.next_ps()
            for mc in range(2):
                P.op('tensor', lambda e, psd=psd, mc=mc, pT=pT: e.matmul(psd[:], lhsT=self.onesb[:], rhs=pT[:, mc, :], start=(mc == 0), stop=(mc == 1)),
                     reads=['onesb', ('pT', h % 2, mc)], writes=[pkd])
            P.op('scalar', lambda e, psd=psd, rden=rden: e.activation(out=rden[:], in_=psd[:], func=AF.Ln), reads=[pkd], writes=[('rden', h % 2)])
            P.op('scalar', lambda e, rden=rden: e.activation(out=rden[:], in_=rden[:], func=AF.Exp, scale=-1.0), reads=[('rden', h % 2)], writes=[('rden', h % 2)])
            for half in range(2):
                c = 2 * h + half
                ps, pk = self.next_ps()
                for mc in range(2):
                    P.op('tensor', lambda e, ps=ps, c=c, mc=mc, pT=pT: e.matmul(ps[:], lhsT=memV[:, mc, c * 128:(c + 1) * 128], rhs=pT[:, mc, :],
                                                                               start=(mc == 0), stop=(mc == 1)),
                         reads=['memV', ('pT', h % 2, mc)], writes=[pk])
                P.op('vector', lambda e, ps=ps, c=c, rden=rden: e.tensor_tensor(out=ob[:, c, :], in0=ps[:], in1=rden[:], op=ALU.mult),
                     reads=[pk, ('rden', h % 2)], writes=[('ob', c)])
        S_(0)
        for h in range(4):
            if h + 1 < 4:
                S_(h + 1)
            E_(h)
            D_(h)
        self.linear(wo, 1, 2, lambda a: ob[:, a, :], lambda a: [('ob', a)], evac_res_for(t))

    def norm2(t):
        self.rmsnorm(x32s[t % 2], xk_for(t), f'mlp_g{l}', hb, lambda a: ('hb', a), self.sqb)

    def mlp1(t):
        def evac_h1(f, ps, pk):
            r = rl[f % 2]
            P.op('scalar', lambda e: e.activation(out=r[:], in_=ps[:], func=AF.Relu), reads=[pk], writes=[('rl', f % 2)])
            P.op('gpsimd', lambda e: e.tensor_tensor(out=h1[:, f, :], in0=r[:], in1=r[:], op=ALU.mult),
                 reads=[('rl', f % 2)], writes=[('h1', f)])
        self.linear(w1, 1, 8, lambda a: hb[:, a, :], lambda a: [('hb', a)], evac_h1)

    def mlp2(t):
        par = t % 2
        x32 = x32s[par]
        self.linear(w2, 4, 2, lambda a: h1[:, a, :], lambda a: [('h1', a)], evac_res_for(t))
        if final:
            self.rmsnorm(x32, xk_for(t), 'fin_g', None, xk_for(t), self.sqb, out32=x32)
        P.op('gpsimd', lambda e: e.dma_start(out=wview(xout, None)[:, :, t * TT:(t + 1) * TT], in_=x32[:]),
             reads=[('x32', par, a) for a in range(8)], writes=[x_out], dma=('xst', par), group=True)

    load_tile(0)
    outproj(0)
    norm1(0)
    for t in range(ntiles):
        nxt = t + 1 < ntiles
        xattn(t)
        norm2(t)
        if nxt:
            outproj(t + 1)
        mlp1(t)
        if nxt:
            norm1(t + 1)
        mlp2(t)
    return P.end()


Builder.dense_phase = _dense_phase


NA_CLS_M = [0, 1, 2, 62, 63]


def _na_geometry():
    p = np.arange(128)
    kr, kc = p // 64, p % 64
    j = np.arange(128)
    qr, qc = j // 64, j % 64
    valid = np.zeros((5, 128, 5, 128), bool)
    rbi = np.zeros((5, 128, 5, 128), np.int64)
    cbi = np.zeros((5, 128, 5, 128), np.int64)
    for ci, m in enumerate(NA_CLS_M):
        b = min(max(2 * m - 4, 0), 118)
        for cc in range(5):
            keyrow = b + 2 * cc + kr
            i = 2 * m + qr
            r0 = np.clip(i - 4, 0, 120)
            rv = (keyrow[:, None] >= r0[None, :]) & (keyrow[:, None] < r0[None, :] + 8)
            qcs = np.clip(qc - 8, 0, 48)
            cvd = (kc[:, None] >= qcs[None, :]) & (kc[:, None] < qcs[None, :] + 16)
            valid[ci, :, cc, :] = rv & cvd
            rbi[ci, :, cc, :] = np.clip(keyrow[:, None] - i[None, :] + 7, 0, 14)
            cbi[ci, :, cc, :] = np.clip(kc[:, None] - qc[None, :], -15, 15) + 15
    return valid, rbi, cbi


def make_consts(inp):
    c = {}
    half = 8
    inv = (500000.0 ** (-np.arange(half) * 2.0 / 16)).astype(np.float32)
    ang = np.arange(S, dtype=np.float32)[:, None] * inv[None, :]
    cos = np.cos(ang).astype(np.float32).T
    sin = np.sin(ang).astype(np.float32).T
    C = np.ones((128, S), np.float32)
    Sn = np.zeros((128, S), np.float32)
    Rm = np.zeros((128, 128), np.float32)
    for hh in range(2):
        for dd in range(16):
            C[hh * 64 + dd] = cos[dd % 8]
            Sn[hh * 64 + dd] = sin[dd % 8]
        for f in range(8):
            Rm[hh * 64 + f + 8, hh * 64 + f] = -1.0
            Rm[hh * 64 + f, hh * 64 + f + 8] = 1.0
    c['ropeC'] = C
    c['ropeS'] = Sn
    c['ropeR'] = Rm
    p = np.arange(128)[:, None]
    j = np.arange(128)[None, :]
    m0 = (p >= j)
    m1 = (p <= j)
    mA = np.zeros((128, 3, 2, 2, 128), np.float32)
    for var in range(3):
        a0 = m0 & (p >= 64) if var == 0 else m0
        a1 = m1 & (p < 64) if var == 2 else m1
        for hh in range(2):
            mA[:, var, hh, 0, :] = a0
            mA[:, var, hh, 1, :] = a1
    c['maskA'] = mA.reshape(128, 3, 512)
    valid, rbi, cbi = _na_geometry()
    c['maskNA'] = np.ascontiguousarray(valid.transpose(1, 0, 2, 3).reshape(128, 5, 640)).astype(np.float32)
    rpb = np.asarray(inp['na_rpb'][0], np.float32)
    eb = rpb[:, rbi, cbi]
    c['ebraw'] = np.ascontiguousarray(eb.transpose(2, 0, 1, 3, 4).reshape(128, 8, 5, 640))
    t = np.arange(S)
    c['rst'] = np.stack([np.broadcast_to((t % 64 != 0).astype(np.float32), (128, S)),
                         np.broadcast_to((t % 64 != 63).astype(np.float32), (128, S))], axis=1).copy()
    s_ = np.arange(64)[:, None]
    t_ = np.arange(64)[None, :]
    c['trim'] = np.stack([(s_ <= t_), (s_ >= t_)], axis=1).astype(np.float32)
    c['ident'] = np.eye(128, dtype=np.float32)
    return c


CONST_SHAPES = {'ropeC': [128, S], 'ropeS': [128, S], 'ropeR': [128, 128], 'maskA': [128, 3, 512],
                'maskNA': [128, 5, 640], 'ebraw': [128, 8, 5, 640], 'rst': [128, 2, S], 'trim': [64, 2, 64], 'ident': [128, 128]}


def _const(self, name):
    return self.dram(name, CONST_SHAPES[name], F32, kind="ExternalInput")


Builder.const = _const


def _qkv0_phase(self, x_in, ntiles=NT):
    P = self.P
    self.phase_common()
    self.norm_bufs()
    w = self.dr['ev_w_in_b0']
    xin = self.dram(x_in, [D, S], F32)
    qk_d = [self.dram(n, [512, S], BF16) for n in ['qA', 'kA', 'qB', 'kB']]
    v_d = self.dram('v_tok', [S, D], BF16)
    ropeC, ropeS = self.const('ropeC'), self.const('ropeS')
    Rm = P.sb([128, 128], BF16)
    P.op('gpsimd', lambda e: e.dma_start(out=Rm[:], in_=self.const('ropeR')), writes=['Rm'], dma='Rm')
    x32s = [P.sb([128, 8, TT], F32) for _ in range(2)]
    cs = [P.sb([128, 2, TT], F32) for _ in range(2)]
    hbs = [P.sb([128, 8, TT], BF16) for _ in range(2)]
    qkst = [P.sb([128, 16, TT], BF16) for _ in range(2)]
    vst = [P.sb([128, 4, D], BF16) for _ in range(2)]
    qs = [P.sb([128, TT], BF16) for _ in range(2)]
    t1 = [P.sb([128, TT], F32) for _ in range(2)]
    t2 = [P.sb([128, TT], F32) for _ in range(2)]

    def load_tile(t):
        par = t % 2
        P.op('sync', lambda e: e.dma_start(out=x32s[par][:], in_=wview(xin, None)[:, :, t * TT:(t + 1) * TT]),
             writes=[('x32', par)], dma=('x32', par))
        P.op('sync', lambda e: e.dma_start(out=cs[par][:, 0, :], in_=ropeC[:, t * TT:(t + 1) * TT]), writes=[('cs', par)], dma=('cs', par))
        P.op('sync', lambda e: e.dma_start(out=cs[par][:, 1, :], in_=ropeS[:, t * TT:(t + 1) * TT]), writes=[('cs', par)], dma=('cs', par), group=True)

    load_tile(0)
    cnt = [0]
    for t in range(ntiles):
        par = t % 2
        x32 = x32s[par]
        if t + 1 < ntiles:
            load_tile(t + 1)
        hb = hbs[par]
        if t == 0:
            self.rmsnorm(x32, lambda a: ('x32', par), 'mix_g0', hb, lambda a, par=par: ('hb', par, a), self.sqb)
        st = qkst[par]
        for gi, fb in enumerate([0, 1, 3, 4]):
            def evac(f, ps, pk, gi=gi, fb=fb, st=st, par=par):
                slot = gi * 4 + (f % 4)
                sc = 0.125 if fb in (0, 3) else 1.0
                if fb >= 3:
                    P.op('scalar', lambda e: e.activation(out=st[:, slot, :], in_=ps[:], func=AF.Copy, scale=sc),
                         reads=[pk], writes=[('qkst', par, slot)])
                    return
                i = cnt[0] % 2
                cnt[0] += 1
                P.op('scalar', lambda e: e.activation(out=qs[i][:], in_=ps[:], func=AF.Copy, scale=sc), reads=[pk], writes=[('qs', i)])
                psr, pkr = self.next_ps()
                P.op('tensor', lambda e: e.matmul(psr[:], lhsT=Rm[:], rhs=qs[i][:], start=True, stop=True), reads=['Rm', ('qs', i)], writes=[pkr])
                P.op('vector', lambda e: e.tensor_tensor(out=t1[i][:], in0=psr[:], in1=cs[par][:, 1, :], op=ALU.mult),
                     reads=[pkr, ('cs', par)], writes=[('t1', i)])
                P.op('gpsimd', lambda e: e.tensor_tensor(out=t2[i][:], in0=qs[i][:], in1=cs[par][:, 0, :], op=ALU.mult),
                     reads=[('qs', i), ('cs', par)], writes=[('t2', i)])
                P.op('vector', lambda e: e.tensor_tensor(out=st[:, slot, :], in0=t1[i][:], in1=t2[i][:], op=ALU.add),
                     reads=[('t1', i), ('t2', i)], writes=[('qkst', par, slot)])
            self.linear(w, 1, None, lambda a, hb=hb: hb[:, a, :], lambda a, par=par: [('hb', par, a)], evac, fbs=[fb])
        for gi in range(4):
            P.op('gpsimd', lambda e, gi=gi, st=st, t=t: e.dma_start(out=wview(qk_d[gi], None)[:, :, t * TT:(t + 1) * TT], in_=st[:, gi * 4:(gi + 1) * 4, :]),
                 reads=[('qkst', par, gi * 4 + k) for k in range(4)], writes=[['qA', 'kA', 'qB', 'kB'][gi]], dma=('qkst', par, gi), group=True)
        if t + 1 < ntiles:
            self.rmsnorm(x32s[1 - par], lambda a, par=par: ('x32', 1 - par), 'mix_g0', hbs[1 - par], lambda a, par=par: ('hb', 1 - par, a), self.sqb)
        vs = vst[par]
        for vi, fb in enumerate([2, 5]):
            wt, wk = self.load_w(w, 0, fb)
            for tc in range(4):
                ps, pk = self.next_ps()
                for a in range(8):
                    P.op('tensor', lambda e, ps=ps, wt=wt, a=a, tc=tc, hb=hb: e.matmul(ps[:], lhsT=hb[:, a, tc * 128:(tc + 1) * 128], rhs=wt[:, a, :],
                                                                               start=(a == 0), stop=(a == 7)),
                         reads=[wk, ('hb', par, a)], writes=[pk])
                eng = 'vector' if tc % 2 else 'scalar'
                if eng == 'vector':
                    P.op('vector', lambda e, ps=ps, tc=tc, vi=vi, vs=vs: e.tensor_copy(out=vs[:, tc, vi * 512:(vi + 1) * 512], in_=ps[:]),
                         reads=[pk], writes=[('vst', par)], group=True)
                else:
                    P.op('scalar', lambda e, ps=ps, tc=tc, vi=vi, vs=vs: e.activation(out=vs[:, tc, vi * 512:(vi + 1) * 512], in_=ps[:], func=AF.Copy),
                         reads=[pk], writes=[('vst', par)], group=True)
        P.op('gpsimd', lambda e, vs=vs, t=t: e.dma_start(out=v_d[t * TT:(t + 1) * TT, :].rearrange("(c p) f -> p c f", p=128), in_=vs[:]),
             reads=[('vst', par)], writes=['v_tok'], dma=('vst', par), group=True)
    return P.end()


Builder.qkv0_phase = _qkv0_phase


def _attA_phase(self, y_out, hps=range(4), branches=(1, 4, 16)):
    P = self.P
    self.phase_common(nring=0, nps=2)
    PAD = 1024
    casts_done = [False]
    qA, kA, v_d = self.dr['qA'], self.dr['kA'], self.dr['v_tok']
    yd = self.dram(y_out, [D, S], BF16)
    pss = [P.ps([128, 1024], F32) for _ in range(3)]
    qT = P.sb([128, S], BF16)
    kT = P.sb([128, S + 2 * PAD + 16], BF16)
    acc = P.sb([128, 2, S], F32)
    vb = [P.sb([128, 80, 128], BF16) for _ in range(2)]
    yb = P.sb([128, S], BF16)
    ex = [P.sb([128, 512], BF16) for _ in range(3)]
    pT = [P.sb([128, 512], BF16) for _ in range(3)]
    mask = P.sb([128, 3, 512], BF16)
    P.op('gpsimd', lambda e: e.dma_start(out=mask[:], in_=self.const('maskA')), writes=['mask'], dma='mask')
    P.op('gpsimd', lambda e: e.memset(kT[:, 0:PAD], 0.0), writes=['kTpad0'])
    P.op('gpsimd', lambda e: e.memset(kT[:, PAD + S:], 0.0), writes=['kTpad1'])
    for i in range(2):
        P.op('gpsimd', lambda e, i=i: e.memset(vb[i][:], 0.0), writes=[('vb', i)])
    vcnt = 0
    ucnt = 0
    for hp in hps:
        P.op('sync', lambda e, hp=hp: e.dma_start(out=qT[:], in_=qA[hp * 128:(hp + 1) * 128, :]), writes=['qT'], dma='qT')
        P.op('sync', lambda e, hp=hp: e.dma_start(out=kT[:, PAD:PAD + S], in_=kA[hp * 128:(hp + 1) * 128, :]), writes=['kT'], dma='kT')
        for r in branches:
            L = S // r
            nb = L // 128
            vi = vcnt % 2
            vcnt += 1
            vbuf = vb[vi]
            first = True
            for rho in range(r):
                cb = rho * (nb + 1)
                base = 64 * r + rho
                src = v_d[base:base + (nb - 1) * 128 * r, hp * 128:(hp + 1) * 128].rearrange("(g p r) f -> p g r f", p=128, r=r)[:, :, 0, :]
                P.op('sync', lambda e, src=src, cb=cb, nb=nb, vbuf=vbuf: e.dma_start(out=vbuf[:, cb + 1:cb + nb, :], in_=src),
                     writes=[('vb', vi)], dma=('vb', vi), group='vfill')
                src0 = v_d[rho:rho + 64 * r:r, hp * 128:(hp + 1) * 128]
                P.op('sync', lambda e, src0=src0, cb=cb, vbuf=vbuf: e.dma_start(out=vbuf[64:128, cb, :], in_=src0),
                     writes=[('vb', vi)], dma=('vb', vi), group='vfill')
                b2 = S - 64 * r + rho
                src1 = v_d[b2:S:r, hp * 128:(hp + 1) * 128]
                P.op('sync', lambda e, src1=src1, cb=cb, nb=nb, vbuf=vbuf: e.dma_start(out=vbuf[0:64, cb + nb, :], in_=src1),
                     writes=[('vb', vi)], dma=('vb', vi), group='vfill')
            units = [(rho, qb) for rho in range(r) for qb in range(nb)]
            if not casts_done[0]:
                casts_done[0] = True
                self.cast_queue([('ev_w_out', 0), ('xa_wq', 0), ('xa_wo', 0), ('mlp_w1', 0), ('mlp_w2', 0)])

            def emit_S(u, idx):
                rho, qb = u
                ps = pss[idx % 3]
                pk = ('pss', idx % 3)
                q0 = qb * 128 * r + rho
                for hh in range(2):
                    for c in range(2):
                        g = qb + c
                        k0 = PAD + (g * 128 - 64) * r + rho
                        P.op('tensor', lambda e, ps=ps, hh=hh, c=c, k0=k0, q0=q0, r=r: e.matmul(
                            ps[:, hh * 512 + c * 128: hh * 512 + (c + 1) * 128],
                            lhsT=kT[hh * 64:(hh + 1) * 64, k0:k0 + 127 * r + 1:r],
                            rhs=qT[hh * 64:(hh + 1) * 64, q0:q0 + 127 * r + 1:r], start=True, stop=True),
                            reads=['kT', 'qT', 'kTpad0', 'kTpad1'], writes=[pk])

            def emit_rest(u, idx):
                rho, qb = u
                i = idx % 3
                ps = pss[i]
                pk = ('pss', i)
                var = 0 if qb == 0 else (2 if qb == nb - 1 else 1)
                psv = ps[:].rearrange("p (h x) -> p h x", h=2)[:, :, 0:256]
                P.op('scalar', lambda e: e.activation(out=ex[i][:].rearrange("p (h x) -> p h x", h=2), in_=psv, func=AF.Exp),
                     reads=[pk], writes=[('ex', i)])
                P.op('vector', lambda e: e.tensor_tensor(out=pT[i][:], in0=ex[i][:], in1=mask[:, var, :], op=ALU.mult),
                     reads=[('ex', i), 'mask'], writes=[('pT', i)])
                return i

            def emit_N(u, idx):
                rho, qb = u
                i = idx % 3
                ps2, pk2 = self.next_ps()
                cb = rho * (nb + 1)
                for hh in range(2):
                    for c in range(2):
                        P.op('tensor', lambda e, hh=hh, c=c, vbuf=vbuf: e.matmul(ps2[hh * 64:(hh + 1) * 64, 0:128], lhsT=vbuf[:, cb + qb + c, hh * 64:(hh + 1) * 64],
                                                                     rhs=pT[i][:, hh * 256 + c * 128: hh * 256 + (c + 1) * 128], start=(c == 0), stop=(c == 1)),
                             reads=[('vb', vi), ('pT', i)], writes=[pk2])
                    for c in range(2):
                        P.op('tensor', lambda e, hh=hh, c=c: e.matmul(ps2[hh * 64:(hh + 1) * 64, 128:256], lhsT=self.onesb[:, 0:64],
                                                                     rhs=pT[i][:, hh * 256 + c * 128: hh * 256 + (c + 1) * 128], start=(c == 0), stop=(c == 1)),
                             reads=['onesb', ('pT', i)], writes=[pk2])
                if idx % 12 == 0:
                    self.emit_cast(after=[pk2])
                q0 = qb * 128 * r + rho
                av = acc[:, :, q0:q0 + 127 * r + 1:r]
                if r == branches[0]:
                    P.op('vector', lambda e: e.tensor_copy(out=av, in_=ps2[:, 0:256].rearrange("p (a b) -> p a b", a=2)),
                         reads=[pk2, 'acc'], writes=['acc'])
                else:
                    P.op('vector', lambda e: e.tensor_tensor(out=av, in0=ps2[:, 0:256].rearrange("p (a b) -> p a b", a=2), in1=av, op=ALU.add),
                         reads=[pk2, 'acc'], writes=['acc'])

            nu = len(units)
            for it in range(nu + 2):
                if it < nu:
                    emit_S(units[it], ucnt + it)
                if 0 <= it - 1 < nu:
                    emit_rest(units[it - 1], ucnt + it - 1)
                if 0 <= it - 2 < nu:
                    emit_N(units[it - 2], ucnt + it - 2)
            ucnt += nu
        P.op('scalar', lambda e: e.activation(out=acc[:, 1, :], in_=acc[:, 1, :], func=AF.Ln), reads=['acc'], writes=['acc'])
        P.op('scalar', lambda e: e.activation(out=acc[:, 1, :], in_=acc[:, 1, :], func=AF.Exp, scale=-1.0), reads=['acc'], writes=['acc'])
        P.op('vector', lambda e: e.tensor_tensor(out=yb[:], in0=acc[:, 0, :], in1=acc[:, 1, :], op=ALU.mult), reads=['acc'], writes=['yb'])
        P.op('sync', lambda e, hp=hp: e.dma_start(out=yd[hp * 128:(hp + 1) * 128, :], in_=yb[:]), reads=['yb'], writes=[y_out], dma='ybst', group='yst')
    self.emit_cast(n=1000)
    return P.end()


def _attB_phase(self, y_out, hps=range(4), ms=range(64)):
    P = self.P
    self.phase_common(nring=0, nps=2)
    qB, kB, v_d = self.dr['qB'], self.dr['kB'], self.dr['v_tok']
    yd = self.dram(y_out, [D, S], BF16)
    pss = [P.ps([128, 1024], F32) for _ in range(3)]
    qTs = [P.sb([128, S], BF16) for _ in range(2)]
    kTs = [P.sb([128, S], BF16) for _ in range(2)]
    vBs = [P.sb([128, 64, 128], BF16) for _ in range(2)]
    yb = P.sb([128, S], BF16)
    EBs = [P.sb([128, 2, 5, 640], BF16) for _ in range(2)]
    mNA = P.sb([128, 5, 640], F32)
    stg = [P.sb([128, 640], F32) for _ in range(2)]
    ex = [P.sb([128, 640], BF16) for _ in range(3)]
    pT = [P.sb([128, 640], BF16) for _ in range(3)]
    rden = P.sb([128, 128], F32)
    ebraw = self.const('ebraw')
    P.op('sync', lambda e: e.dma_start(out=mNA[:], in_=self.const('maskNA')), writes=['mNA'], dma='mNA')
    ucnt = 0
    hpl = list(hps)

    def load_hp(hpi):
        hp_, bs_ = hpl[hpi], hpi % 2
        P.op('sync', lambda e: e.dma_start(out=qTs[bs_][:], in_=qB[hp_ * 128:(hp_ + 1) * 128, :]), writes=[('qT', bs_)], dma=('qT', bs_))
        P.op('sync', lambda e: e.dma_start(out=kTs[bs_][:], in_=kB[hp_ * 128:(hp_ + 1) * 128, :]), writes=[('kT', bs_)], dma=('kT', bs_))
        P.op('sync', lambda e: e.dma_start(out=vBs[bs_][:], in_=v_d[:, 512 + hp_ * 128:512 + (hp_ + 1) * 128].rearrange("(g p) f -> p g f", p=128)),
             writes=[('vB', bs_)], dma=('vB', bs_))

    def eb_dma(hpi, j):
        hp_, hh_, cls_, si = hpl[hpi], j // 5, j % 5, j % 2
        P.op('sync', lambda e: e.dma_start(out=stg[si][:], in_=ebraw[:, hp_ * 2 + hh_, cls_, :]), writes=[('stg', si)], dma=('stg', si))

    def eb_compute(hpi, j):
        bs_, hh_, cls_, si = hpi % 2, j // 5, j % 5, j % 2
        P.op('scalar', lambda e: e.activation(out=stg[si][:], in_=stg[si][:], func=AF.Exp), reads=[('stg', si)], writes=[('stg', si)])
        P.op('vector', lambda e: e.tensor_tensor(out=EBs[bs_][:, hh_, cls_, :], in0=stg[si][:], in1=mNA[:, cls_, :], op=ALU.mult),
             reads=[('stg', si), 'mNA'], writes=[('EB', bs_)], group='ebfill')

    load_hp(0)
    for j in range(10):
        eb_dma(0, j)
        eb_compute(0, j)
    for hpi, hp in enumerate(hpl):
        bs = hpi % 2
        qT, kT, vB, EB = qTs[bs], kTs[bs], vBs[bs], EBs[bs]
        kq, kk, kv, keb = ('qT', bs), ('kT', bs), ('vB', bs), ('EB', bs)
        units = [(m, hh) for m in ms for hh in range(2)]
        if hpi == 0:
            self.cast_queue([('od_w_in', 0), ('od_w_out', 0), ('xa_wq', 1), ('xa_wo', 1), ('mlp_w1', 1), ('mlp_w2', 1)])

        def geom(m):
            b = min(max(2 * m - 4, 0), 118)
            cls = 0 if m == 0 else (1 if m == 1 else (3 if m == 62 else (4 if m == 63 else 2)))
            return b // 2, cls

        def emit_S(u, idx):
            m, hh = u
            g0, cls = geom(m)
            ps = pss[idx % 3]
            pk = ('pss', idx % 3)
            kT_, qT_, kk_, kq_ = kT, qT, kk, kq
            for cc in range(5):
                P.op('tensor', lambda e, cc=cc: e.matmul(ps[:, cc * 128:(cc + 1) * 128], lhsT=kT_[hh * 64:(hh + 1) * 64, (g0 + cc) * 128:(g0 + cc + 1) * 128],
                                                        rhs=qT_[hh * 64:(hh + 1) * 64, m * 128:(m + 1) * 128], start=True, stop=True),
                     reads=[kk_, kq_], writes=[pk])

        def emit_rest(u, idx):
            m, hh = u
            g0, cls = geom(m)
            i = idx % 3
            EB_, keb_ = EB, keb
            P.op('scalar', lambda e: e.activation(out=ex[i][:], in_=pss[i][:, 0:640], func=AF.Exp), reads=[('pss', i)], writes=[('ex', i)])
            P.op('vector', lambda e: e.tensor_tensor(out=pT[i][:], in0=ex[i][:], in1=EB_[:, hh, cls, :], op=ALU.mult),
                 reads=[('ex', i), keb_], writes=[('pT', i)])

        ps2h = {}

        def emit_N(u, idx):
            m, hh = u
            g0, cls = geom(m)
            i = idx % 3
            if hh == 0:
                ps2h[m] = self.next_ps()
            ps2, pk2 = ps2h[m]
            vB_, kv_ = vB, kv
            if idx % 7 == 0:
                self.emit_cast(after=[('pT', i)])
            for cc in range(5):
                P.op('tensor', lambda e, cc=cc: e.matmul(ps2[hh * 64:(hh + 1) * 64, 0:128], lhsT=vB_[:, g0 + cc, hh * 64:(hh + 1) * 64],
                                                        rhs=pT[i][:, cc * 128:(cc + 1) * 128], start=(cc == 0), stop=(cc == 4)),
                     reads=[kv_, ('pT', i)], writes=[pk2])
            for cc in range(5):
                P.op('tensor', lambda e, cc=cc: e.matmul(ps2[hh * 64:(hh + 1) * 64, 128:256], lhsT=self.onesb[:, 0:64],
                                                        rhs=pT[i][:, cc * 128:(cc + 1) * 128], start=(cc == 0), stop=(cc == 4)),
                     reads=['onesb', ('pT', i)], writes=[pk2])
            if hh == 1:
                def fin(m=m, ps2=ps2, pk2=pk2):
                    P.op('scalar', lambda e: e.activation(out=rden[:], in_=ps2[:, 128:256], func=AF.Ln), reads=[pk2], writes=['rden'])
                    P.op('scalar', lambda e: e.activation(out=rden[:], in_=rden[:], func=AF.Exp, scale=-1.0), reads=['rden'], writes=['rden'])
                    P.op('vector', lambda e: e.tensor_tensor(out=yb[:, m * 128:(m + 1) * 128], in0=ps2[:, 0:128], in1=rden[:], op=ALU.mult),
                         reads=[pk2, 'rden'], writes=['yb'], group='ybfill')
                pendF.append(fin)

        nu = len(units)
        pendF = []
        for it in range(nu + 2):
            if hpi + 1 < len(hpl):
                if it == 4:
                    load_hp(hpi + 1)
                if it >= 10 and (it - 10) % 10 == 0 and (it - 10) // 10 < 10:
                    eb_dma(hpi + 1, (it - 10) // 10)
                if it >= 16 and (it - 16) % 10 == 0 and (it - 16) // 10 < 10:
                    eb_compute(hpi + 1, (it - 16) // 10)
            if it < nu:
                emit_S(units[it], ucnt + it)
            if 0 <= it - 1 < nu:
                emit_rest(units[it - 1], ucnt + it - 1)
            old_pend, pendF = pendF, []
            for f_ in old_pend:
                f_()
            if 0 <= it - 2 < nu:
                emit_N(units[it - 2], ucnt + it - 2)
        for f_ in pendF:
            f_()
        pendF = []
        ucnt += nu
        P.op('sync', lambda e, hp=hp: e.dma_start(out=yd[512 + hp * 128:512 + (hp + 1) * 128, :], in_=yb[:]), reads=['yb'], writes=[y_out], dma='ybst', group='yst')
    self.emit_cast(n=1000)
    return P.end()


Builder.attA_phase = _attA_phase
Builder.attB_phase = _attB_phase


def _lb_consts(self):
    P = self.P
    o, _ = CV['lb_logit']
    lb = P.sb([128, 8], F32)
    oml = P.sb([128, 8], F32)
    P.op('vector', lambda e: e.tensor_tensor(out=lb[:], in0=self.cv[:, o + 8:o + 16], in1=self.cv[:, o:o + 8], op=ALU.subtract),
         reads=['cv'], writes=['lb'])
    P.op('scalar', lambda e: e.activation(out=lb[:], in_=lb[:], func=AF.Sigmoid), reads=['lb'], writes=['lb'])
    P.op('vector', lambda e: e.tensor_scalar(out=oml[:], in0=lb[:], scalar1=-1.0, scalar2=1.0, op0=ALU.mult, op1=ALU.add),
         reads=['lb'], writes=['oml'])
    return lb, oml


def _proj1_phase(self, x_in, ntiles=NT):
    P = self.P
    self.phase_common()
    self.norm_bufs()
    w = self.dr['od_w_in_b0']
    xin = self.dram(x_in, [D, S], F32)
    u_d = self.dram('u1', [512, S], F32)
    gg_d = self.dram('gg1', [512, S], BF16)
    qs_d = self.dram('qs1', [512, S], BF16)
    ff_d = [self.dram(f'ff1_{d_}', [512, S], F32) for d_ in range(2)]
    gs_d = self.dram('gs1', [512, S], BF16)
    v_d = self.dram('v1_tok', [S, 512], BF16)
    lb, oml = self.lb_consts()
    x32s = [P.sb([128, 8, TT], F32) for _ in range(2)]
    hbs = [P.sb([128, 8, TT], BF16) for _ in range(2)]
    ust = [P.sb([128, 4, TT], F32) for _ in range(2)]
    fst = [[P.sb([128, 4, TT], F32) for _ in range(2)] for _ in range(2)]
    bst = [P.sb([128, 3, 4, TT], BF16) for _ in range(2)]
    vst = [P.sb([128, 4, 512], BF16) for _ in range(2)]
    sg = [P.sb([128, TT], F32) for _ in range(2)]

    def load_tile(t):
        par = t % 2
        P.op('sync', lambda e: e.dma_start(out=x32s[par][:], in_=wview(xin, None)[:, :, t * TT:(t + 1) * TT]),
             writes=[('x32', par)], dma=('x32', par))

    load_tile(0)
    cnt = [0]
    for t in range(ntiles):
        par = t % 2
        x32 = x32s[par]
        if t + 1 < ntiles:
            load_tile(t + 1)
        hb = hbs[par]
        if t == 0:
            self.rmsnorm(x32, lambda a: ('x32', par), 'mix_g1', hb, lambda a, par=par: ('hb', par, a), self.sqb)

        def evac(f, ps, pk, par=par):
            fb, c = f // 4, f % 4
            if fb == 0:
                P.op('vector', lambda e: e.tensor_copy(out=ust[par][:, c, :], in_=ps[:]), reads=[pk], writes=[('ust', par)], group='fill')
            elif fb in (1, 2, 6):
                k = {1: 0, 2: 1, 6: 2}[fb]
                fn = AF.Gelu_apprx_tanh if fb == 1 else AF.Silu
                P.op('scalar', lambda e: e.activation(out=bst[par][:, k, c, :], in_=ps[:], func=fn), reads=[pk], writes=[('bst', par, k)], group='fill')
            else:
                d_ = fb - 3
                i = cnt[0] % 2
                cnt[0] += 1
                P.op('scalar', lambda e: e.activation(out=sg[i][:], in_=ps[:], func=AF.Sigmoid), reads=[pk], writes=[('sg', i)])
                col = d_ * 4 + c
                P.op('vector', lambda e: e.tensor_scalar(out=fst[par][d_][:, c, :], in0=sg[i][:], scalar1=oml[:, col:col + 1], scalar2=lb[:, col:col + 1],
                                                         op0=ALU.mult, op1=ALU.add),
                     reads=[('sg', i), 'lb', 'oml'], writes=[('fst', par, d_)], group='fill')
        self.linear(w, 1, None, lambda a, hb=hb: hb[:, a, :], lambda a, par=par: [('hb', par, a)], evac, fbs=[0, 1, 2, 3, 4, 6])
        if t + 1 < ntiles:
            self.rmsnorm(x32s[1 - par], lambda a, par=par: ('x32', 1 - par), 'mix_g1', hbs[1 - par], lambda a, par=par: ('hb', 1 - par, a), self.sqb)
        wt, wk = self.load_w(w, 0, 5)
        for tc in range(4):
            ps, pk = self.next_ps()
            for a in range(8):
                P.op('tensor', lambda e, ps=ps, wt=wt, a=a, tc=tc, hb=hb: e.matmul(ps[:], lhsT=hb[:, a, tc * 128:(tc + 1) * 128], rhs=wt[:, a, :],
                                                                           start=(a == 0), stop=(a == 7)),
                     reads=[wk, ('hb', par, a)], writes=[pk])
            P.op('vector', lambda e, ps=ps, tc=tc, par=par: e.tensor_copy(out=vst[par][:, tc, :], in_=ps[:]), reads=[pk], writes=[('vst', par)], group='fill')
        sl = slice(t * TT, (t + 1) * TT)
        P.op('gpsimd', lambda e, par=par, sl=sl: e.dma_start(out=wview(u_d, None)[:, :, sl], in_=ust[par][:]), reads=[('ust', par)], writes=['u1'],
             dma=('ust', par), group='st')
        for k, dd in enumerate([gg_d, qs_d, gs_d]):
            P.op('gpsimd', lambda e, par=par, sl=sl, k=k, dd=dd: e.dma_start(out=wview(dd, None)[:, :, sl], in_=bst[par][:, k, :, :]),
                 reads=[('bst', par, k)], writes=[('bd', k)], dma=('bst', par, k), group='st')
        for d_ in range(2):
            P.op('gpsimd', lambda e, par=par, sl=sl, d_=d_: e.dma_start(out=wview(ff_d[d_], None)[:, :, sl], in_=fst[par][d_][:]),
                 reads=[('fst', par, d_)], writes=[('ffd', d_)], dma=('fst', par, d_), group='st')
        P.op('gpsimd', lambda e, par=par, t=t: e.dma_start(out=v_d[t * TT:(t + 1) * TT, :].rearrange("(c p) f -> p c f", p=128), in_=vst[par][:]),
             reads=[('vst', par)], writes=['v1_tok'], dma=('vst', par), group='st')
    return P.end()


Builder.lb_consts = _lb_consts
Builder.proj1_phase = _proj1_phase


def _lru_phase(self, y_out, chunks=range(4), stage=9):
    P = self.P
    self.phase_common(nring=0)
    u_d, gg_d = self.dr['u1'], self.dr['gg1']
    yd = self.dram(y_out, [D, S], BF16)
    wa_d = self.dram('lru_wa', [1, 2, 8, 64, 64], F32, kind="ExternalInput")
    wx_d = self.dram('lru_wx', [1, 2, 8, 64, 64], F32, kind="ExternalInput")
    o_lam, o_ba, o_bx, o_cw, o_cb = CV['lru_lam'][0], CV['lru_ba'][0], CV['lru_bx'][0], CV['conv_w'][0], CV['conv_b'][0]
    cv = self.cv
    oneb = P.sb([128, 1], F32)
    P.op('vector', lambda e: e.memset(oneb[:], 1.0), writes=['oneb'])
    c1 = P.sb([128, 8], F32)
    c2 = P.sb([128, 8], F32)
    P.op('scalar', lambda e: e.activation(out=c1[:], in_=cv[:, o_lam:o_lam + 8], func=AF.Exp, scale=-1.0), reads=['cv'], writes=['c1'])
    P.op('scalar', lambda e: e.activation(out=c1[:], in_=c1[:], func=AF.Ln, bias=oneb[:, 0:1]), reads=['c1', 'oneb'], writes=['c1'])
    P.op('vector', lambda e: e.tensor_scalar(out=c2[:], in0=c1[:], scalar1=-16.0, scalar2=None, op0=ALU.mult), reads=['c1'], writes=['c2'])
    P.op('vector', lambda e: e.tensor_scalar(out=c1[:], in0=c1[:], scalar1=-8.0, scalar2=None, op0=ALU.mult), reads=['c1', 'c2'], writes=['c1'])
    Wg = P.sb([128, 16, 128], BF16)
    W32 = [P.sb([128, 128], F32) for _ in range(2)]
    for i in range(2):
        P.op('vector', lambda e, i=i: e.memset(W32[i][:], 0.0), writes=[('W32', i)])
    n = 0
    for d_ in range(2):
        for gi, wd in enumerate([wa_d, wx_d]):
            for c in range(4):
                i = n % 2
                n += 1
                P.op('sync', lambda e, i=i, wd=wd, d_=d_, c=c: e.dma_start(out=W32[i][0:64, 0:64], in_=wd[0, d_, 2 * c]),
                     writes=[('W32', i)], dma=('W32', i))
                P.op('sync', lambda e, i=i, wd=wd, d_=d_, c=c: e.dma_start(out=W32[i][64:128, 64:128], in_=wd[0, d_, 2 * c + 1]),
                     writes=[('W32', i)], dma=('W32', i), group='w')
                P.op('scalar', lambda e, i=i, idx=(d_ * 2 + gi) * 4 + c: e.activation(out=Wg[:, idx, :], in_=W32[i][:], func=AF.Copy),
                     reads=[('W32', i)], writes=['Wg'], group='wg')
    if stage < 2:
        return P.end()
    bufX = P.sb([128, S + 3], F32)
    uc = P.sb([128, S], F32)
    ubf = P.sb([128, S], BF16)
    ggb = P.sb([128, S], BF16)
    yb = P.sb([128, S], BF16)
    PW = 2048
    Rs = [P.sb([128, PW], F32) for _ in range(2)]
    Is = [P.sb([128, PW], F32) for _ in range(2)]
    As = [P.sb([128, PW], F32) for _ in range(2)]
    S2s = [P.sb([128, PW], F32) for _ in range(2)]
    pcnt = [0]
    Bv = P.sb([128, PW], F32)
    H2 = P.sb([128, PW], F32)
    hst = P.sb([128, 1], F32)
    P.op('gpsimd', lambda e: e.memset(bufX[:, 0:2], 0.0), writes=['bufXp'])
    P.op('gpsimd', lambda e: e.memset(bufX[:, S + 2:S + 3], 0.0), writes=['bufXp'], group='p')
    Hs = bufX[:, 2:2 + S]
    for c in chunks:
        P.op('sync', lambda e, c=c: e.dma_start(out=bufX[:, 2:2 + S], in_=u_d[c * 128:(c + 1) * 128, :]), writes=['bufX'], dma='bufX')
        P.op('sync', lambda e, c=c: e.dma_start(out=ggb[:], in_=gg_d[c * 128:(c + 1) * 128, :]), writes=['ggb'], dma='ggb')
        for cp in range(S // PW):
            p0 = cp * PW
            P.op('vector', lambda e, c=c, p0=p0: e.tensor_scalar(out=uc[:, p0:p0 + PW], in0=bufX[:, p0:p0 + PW], scalar1=cv[:, o_cw + c:o_cw + c + 1],
                                                                 scalar2=cv[:, o_cb + c:o_cb + c + 1], op0=ALU.mult, op1=ALU.add),
                 reads=['bufX', 'bufXp', 'cv'], writes=[('uc', cp)])
            for j in range(1, 4):
                P.op('vector', lambda e, c=c, j=j, p0=p0: e.scalar_tensor_tensor(out=uc[:, p0:p0 + PW], in0=bufX[:, p0 + j:p0 + j + PW],
                                                                                 scalar=cv[:, o_cw + j * 4 + c:o_cw + j * 4 + c + 1],
                                                                                 in1=uc[:, p0:p0 + PW], op0=ALU.mult, op1=ALU.add),
                     reads=['bufX', 'bufXp', 'cv', ('uc', cp)], writes=[('uc', cp)])
            P.op('gpsimd', lambda e, p0=p0: e.tensor_copy(out=ubf[:, p0:p0 + PW], in_=uc[:, p0:p0 + PW]), reads=[('uc', cp)], writes=[('ubf', cp)])
        if stage < 3:
            continue
        for d_ in range(2):
            col = d_ * 4 + c
            order = range(4) if d_ == 0 else range(3, -1, -1)
            for pi, pc in enumerate(order):
                c0 = pc * PW
                si = pcnt[0] % 2
                pcnt[0] += 1
                R, I, A, S2 = Rs[si], Is[si], As[si], S2s[si]
                kR, kI, kA, kS2 = ('R', si), ('I', si), ('A', si), ('S2', si)
                for q in range(PW // 512):
                    for gi, (dst, dk, ob) in enumerate([(R, kR, o_ba), (I, kI, o_bx)]):
                        ps, pk = self.next_ps()
                        P.op('tensor', lambda e, ps=ps, idx=(d_ * 2 + gi) * 4 + c, s0=c0 + q * 512: e.matmul(ps[:], lhsT=Wg[:, idx, :], rhs=ubf[:, s0:s0 + 512],
                                                                                                         start=True, stop=True),
                             reads=['Wg', ('ubf', pc)], writes=[pk])
                        P.op('scalar', lambda e, ps=ps, dst=dst, q=q, bc=ob + col: e.activation(out=dst[:, q * 512:(q + 1) * 512], in_=ps[:], func=AF.Sigmoid,
                                                                                               bias=cv[:, bc:bc + 1]),
                             reads=[pk, 'cv'], writes=[dk], group='g')
                if stage < 4:
                    continue
                P.op('scalar', lambda e, col=col, S2=S2, R=R: e.activation(out=S2[:], in_=R[:], func=AF.Exp, scale=c2[:, col:col + 1]), reads=[kR, 'c2'], writes=[kS2])
                P.op('scalar', lambda e, col=col, A=A, R=R: e.activation(out=A[:], in_=R[:], func=AF.Exp, scale=c1[:, col:col + 1]), reads=[kR, 'c1'], writes=[kA])
                P.op('scalar', lambda e, S2=S2: e.activation(out=S2[:], in_=S2[:], func=AF.Sqrt, scale=-1.0, bias=oneb[:, 0:1]), reads=[kS2, 'oneb'], writes=[kS2])
                P.op('gpsimd', lambda e, c0=c0, I=I: e.tensor_tensor(out=I[:], in0=I[:], in1=uc[:, c0:c0 + PW], op=ALU.mult), reads=[kI, ('uc', pc)], writes=[kI])
                P.op('vector', lambda e, S2=S2, I=I: e.tensor_tensor(out=Bv[:], in0=S2[:], in1=I[:], op=ALU.mult), reads=[kS2, kI], writes=['Bv'])
                if stage < 5:
                    continue
                if d_ == 0:
                    init = 0.0 if (pi == 0 or stage == 5) else bufX[:, 2 + c0 - 1:2 + c0]
                    P.op('vector', lambda e, c0=c0, init=init, A=A: e.tensor_tensor_scan(out=bufX[:, 2 + c0:2 + c0 + PW], data0=A[:], data1=Bv[:], initial=init,
                                                                                  op0=ALU.mult, op1=ALU.add), reads=[kA, 'Bv', 'bufX'], writes=['bufX'])
                elif stage >= 7:
                    if pi > 0:
                        P.op('vector', lambda e: e.tensor_copy(out=hst[:], in_=H2[:, 0:1]), reads=['H2'], writes=['hst'])
                    init = 0.0 if (pi == 0 or stage == 7) else hst[:, 0:1]
                    P.op('vector', lambda e, init=init, A=A: e.tensor_tensor_scan(out=H2[:, PW - 1::-1], data0=A[:, PW - 1::-1], data1=Bv[:, PW - 1::-1], initial=init,
                                                                           op0=ALU.mult, op1=ALU.add), reads=[kA, 'Bv', 'hst'], writes=['H2'])
                    if stage >= 9:
                        P.op('vector', lambda e, c0=c0: e.tensor_tensor(out=bufX[:, 2 + c0:2 + c0 + PW], in0=bufX[:, 2 + c0:2 + c0 + PW], in1=H2[:], op=ALU.add),
                             reads=['bufX', 'H2'], writes=['bufX'])
        P.op('vector', lambda e: e.tensor_tensor(out=yb[:], in0=Hs, in1=ggb[:], op=ALU.mult), reads=['bufX', 'ggb'], writes=['yb'])
        P.op('sync', lambda e, c=c: e.dma_start(out=yd[c * 128:(c + 1) * 128, :], in_=yb[:]), reads=['yb'], writes=[y_out], dma='ybst', group='yst')
    return P.end()


Builder.lru_phase = _lru_phase


def _hgrn_phase(self, y_out, heads=range(4), dirs=(0, 1), stage=9, dbg=''):
    P = self.P
    self.phase_common(nring=0, nps=1)
    self.norm_bufs()
    qs_d, gs_d, v_d = self.dr['qs1'], self.dr['gs1'], self.dr['v1_tok']
    ff_d = [self.dr['ff1_0'], self.dr['ff1_1']]
    yd = self.dram(y_out, [D, S], BF16)
    patt = [P.ps([128, 512], F32) for _ in range(2)]
    po = [P.ps([128, 512], F32) for _ in range(2)]
    pstt = [P.ps([128, 512], F32) for _ in range(2)]
    ptr = P.ps([128, 1024], BF16)
    PW = 512
    NSET = 4
    qs = P.sb([128, S], BF16)
    vt = P.sb([128, 64, 128], BF16)
    msk = P.sb([128, PW + 1], BF16)
    Fs = [P.sb([128, PW], F32) for _ in range(NSET)]
    LFs = [P.sb([128, PW], F32) for _ in range(NSET)]
    Gs = [P.sb([128, PW], F32) for _ in range(NSET)]
    pcn = [0]
    qe = P.sb([128, S], BF16)
    ke = P.sb([128, S], BF16)
    kd = [P.sb([128, 512], BF16) for _ in range(2)]
    kdTs = [P.sb([128, 64, 128], BF16) for _ in range(2)]
    egl = P.sb([128, 128], F32)
    O = P.sb([128, S], F32)
    attb = [P.sb([128, 512], BF16) for _ in range(2)]
    tri = P.sb([128, 2, 512], BF16)
    ident = P.sb([128, 128], BF16)
    S32 = [P.sb([128, 128], F32) for _ in range(2)]
    Sbf = [P.sb([128, 128], BF16) for _ in range(2)]
    yst = [P.sb([128, 512], BF16) for _ in range(2)]
    tmpn = P.sb([128, 512], F32)
    P.op('gpsimd', lambda e: e.dma_start(out=ident[:], in_=self.const('ident')), writes=['ident'], dma='ident')
    trim = self.const('trim')
    P.op('vector', lambda e: e.memset(tri[:], 0.0), writes=['tri'])
    for half in range(2):
        for blk in range(4):
            P.op('gpsimd', lambda e, half=half, blk=blk: e.dma_start(
                out=tri[half * 64:(half + 1) * 64, :, blk * 128 + half * 64:blk * 128 + half * 64 + 64], in_=trim),
                writes=['tri'], dma='tri', group='tri')
    for i in range(2):
        P.op('vector', lambda e, i=i: e.memset(patt[i][:], 0.0), writes=['pattz'], group='pz')
    hcnt = [0, 0]
    P.op('vector', lambda e: e.memset(msk[:], 1.0), writes=['msk'])
    P.op('vector', lambda e: e.memset(msk[:, 0:PW + 1:64], 0.0), reads=['msk'], writes=['msk'])
    for i in range(2):
        P.op('gpsimd', lambda e, i=i: e.memset(kdTs[i][:], 0.0), writes=['kdTz'], group='kz')
    og = CV['gnorm'][0]
    if stage < 2:
        return P.end()
    acnt = 0
    ocnt = 0
    scnt = 0
    kcnt = [0]
    ycnt = 0
    first_dir = dirs[0]
    for hd in heads:
        P.op('sync', lambda e, hd=hd: e.dma_start(out=qs[:], in_=qs_d[hd * 128:(hd + 1) * 128, :]), writes=['qs'], dma='qs')
        P.op('sync', lambda e, hd=hd: e.dma_start(out=vt[:], in_=v_d[:, hd * 128:(hd + 1) * 128].rearrange("(g p) f -> p g f", p=128)), writes=['vt'], dma='vt')
        if 'readvt' in dbg:
            P.op('vector', lambda e: e.tensor_copy(out=tmpn[:, 0:128], in_=vt[:, 0, :]), reads=['vt'], writes=['tmpn'])
        for d_ in dirs:
            def g1(pc, hd=hd, d_=d_):
                c0 = pc * PW
                fi = pc % NSET
                F, LF, G = Fs[fi], LFs[fi], Gs[fi]
                kF, kLF, kG = ('F', fi), ('LF', fi), ('G', fi)
                P.op('sync', lambda e: e.dma_start(out=F[:], in_=ff_d[d_][hd * 128:(hd + 1) * 128, c0:c0 + PW]), writes=[kF], dma=kF)
                P.op('scalar', lambda e: e.activation(out=LF[:], in_=F[:], func=AF.Ln), reads=[kF], writes=[kLF])
                P.op('gpsimd', lambda e: e.tensor_scalar(out=F[:], in0=F[:], scalar1=-1.0, scalar2=1.0, op0=ALU.mult, op1=ALU.add), reads=[kF, kLF], writes=[kF])
                if d_ == 0:
                    P.op('vector', lambda e: e.tensor_tensor_scan(out=G[:], data0=msk[:, 0:PW], data1=LF[:], initial=0.0, op0=ALU.mult, op1=ALU.add),
                         reads=['msk', kLF], writes=[kG])
                else:
                    P.op('vector', lambda e: e.tensor_tensor_scan(out=G[:, PW - 1::-1], data0=msk[:, PW:0:-1], data1=LF[:, PW - 1::-1], initial=0.0,
                                                                  op0=ALU.mult, op1=ALU.add), reads=['msk', kLF], writes=[kG])

            def g2(pc, hd=hd, d_=d_):
                c0 = pc * PW
                fi = pc % NSET
                F, LF, G = Fs[fi], LFs[fi], Gs[fi]
                kF, kLF, kG = ('F', fi), ('LF', fi), ('G', fi)
                P.op('scalar', lambda e: e.activation(out=LF[:], in_=G[:], func=AF.Exp), reads=[kG], writes=[kLF])
                P.op('vector', lambda e: e.tensor_tensor(out=qe[:, c0:c0 + PW], in0=qs[:, c0:c0 + PW], in1=LF[:], op=ALU.mult),
                     reads=['qs', kLF], writes=[('qe', pc // 2)], group='pf')
                e0 = 63 if d_ == 0 else 0
                nch = PW // 64
                P.op('scalar', lambda e: e.activation(out=egl[:, pc * nch:(pc + 1) * nch], in_=LF[:, e0:PW:64], func=AF.Copy), reads=[kLF], writes=[('egl', pc // 2)], group='pf')

            def g3(pc, hd=hd, d_=d_):
                c0 = pc * PW
                fi = pc % NSET
                F, LF, G = Fs[fi], LFs[fi], Gs[fi]
                kF, kLF, kG = ('F', fi), ('LF', fi), ('G', fi)
                P.op('scalar', lambda e: e.activation(out=LF[:], in_=G[:], func=AF.Exp, scale=-1.0), reads=[kG, kLF], writes=[kLF])
                P.op('gpsimd', lambda e: e.tensor_tensor(out=ke[:, c0:c0 + PW], in0=F[:], in1=LF[:], op=ALU.mult), reads=[kF, kLF], writes=[('ke', pc // 2)], group='pf')

            def g4(pc, hd=hd, d_=d_):
                c0 = pc * PW
                ki = kcnt[0] % 2
                kcnt[0] += 1
                ch0 = c0 // 64
                nch = PW // 64
                P.op('vector', lambda e: e.tensor_tensor(
                    out=kd[ki][:].rearrange("p (c t) -> p c t", t=64), in0=ke[:, c0:c0 + PW].rearrange("p (c t) -> p c t", t=64),
                    in1=egl[:, ch0:ch0 + nch].unsqueeze(2).to_broadcast([128, nch, 64]), op=ALU.mult), reads=[('ke', pc // 2), ('egl', pc // 2)], writes=[('kd', ki)])
                ph = pc % 2
                for j in range(4):
                    P.op('tensor', lambda e, j=j: e.transpose(out=ptr[:, ph * 512 + j * 128:ph * 512 + (j + 1) * 128], in_=kd[ki][:, j * 128:(j + 1) * 128], identity=ident[:]),
                         reads=[('kd', ki), 'ident'], writes=[('ptr', ph)])
                b0 = c0 // 128
                P.op('scalar', lambda e: e.activation(out=kdTs[0][0:64, b0:b0 + 4, :], in_=ptr[0:64, ph * 512:(ph + 1) * 512].rearrange("p (b d) -> p b d", d=128), func=AF.Copy),
                     reads=[('ptr', ph), 'kdTz'], writes=[('kdT', pc // 2)], group='pf')
                P.op('vector', lambda e: e.tensor_copy(out=kdTs[1][64:128, b0:b0 + 4, :], in_=ptr[64:128, ph * 512:(ph + 1) * 512].rearrange("p (b d) -> p b d", d=128)),
                     reads=[('ptr', ph), 'kdTz'], writes=[('kdT', pc // 2)], group='pf')

            npc_ = S // PW
            for st_ in range(npc_ + 3):
                if st_ < npc_:
                    g1(st_)
                if 0 <= st_ - 1 < npc_:
                    g2(st_ - 1)
                if 0 <= st_ - 2 < npc_:
                    g3(st_ - 2)
                if 0 <= st_ - 3 < npc_:
                    g4(st_ - 3)
            if stage < 5:
                continue
            P.op('vector', lambda e: e.memset(S32[1][:], 0.0), writes=[('S32', 1)])
            cnt = 0
            batches = list(range(16)) if d_ == 0 else list(range(15, -1, -1))
            for bt in batches:
                b0 = bt * 4
                ai = acnt % 2
                acnt += 1
                for j in range(4):
                    for half in range(2):
                        n = 2 * (b0 + j) + half
                        c0_ = j * 128 + half * 64
                        P.op('tensor', lambda e, ai=ai, c0_=c0_, half=half, n=n: e.matmul(patt[ai][half * 64:(half + 1) * 64, c0_:c0_ + 64],
                                                                                         lhsT=ke[:, n * 64:(n + 1) * 64], rhs=qe[:, n * 64:(n + 1) * 64], start=True, stop=True),
                             reads=[('ke', n // 16), ('qe', n // 16), 'pattz'], writes=[('patt', ai)])
                P.op('vector', lambda e, ai=ai, d_=d_: e.tensor_tensor(out=attb[ai][:], in0=patt[ai][:], in1=tri[:, d_, :], op=ALU.mult),
                     reads=[('patt', ai), 'tri'], writes=[('attb', ai)])
                if stage < 6:
                    continue
                oi = ocnt % 2
                ocnt += 1
                blocks = list(range(b0, b0 + 4))
                if d_ == 1:
                    blocks = blocks[::-1]
                pinfo = {}
                for blk in blocks:
                    for n in ([2 * blk, 2 * blk + 1] if d_ == 0 else [2 * blk + 1, 2 * blk]):
                        half = n % 2
                        sslot = hcnt[half] % 4
                        hcnt[half] += 1
                        pst_t = pstt[half]
                        pkey = 'pstb' if 'nohoist' not in dbg else ('pst', half, sslot)
                        pinfo[n] = (pst_t, sslot, pkey)
                        if 'nohoist' not in dbg:
                          P.op('tensor', lambda e, pst_t=pst_t, sslot=sslot, half=half, blk=blk: e.matmul(
                            pst_t[:, sslot * 128:(sslot + 1) * 128], lhsT=kdTs[half][:, blk, :], rhs=vt[:, blk, :],
                            start=True, stop=True), reads=[('kdT', blk // 8), 'vt'], writes=[pkey])
                for blk in blocks:
                    j = blk - b0
                    P.op('tensor', lambda e, oi=oi, j=j, blk=blk, ai=ai: e.matmul(po[oi][:, j * 128:(j + 1) * 128], lhsT=vt[:, blk, :], rhs=attb[ai][:, j * 128:(j + 1) * 128],
                                                                                 start=True, stop=False), reads=['vt', ('attb', ai)], writes=[('po', oi)])
                    chunks = [2 * blk, 2 * blk + 1] if d_ == 0 else [2 * blk + 1, 2 * blk]
                    for ci, n in enumerate(chunks):
                        half = n % 2
                        pst_t, sslot, pkey = pinfo[n]
                        cur, prev = cnt % 2, (cnt - 1) % 2
                        if 'nohoist' in dbg:
                            P.op('tensor', lambda e, pst_t=pst_t, sslot=sslot, half=half, blk=blk: e.matmul(
                                pst_t[:, sslot * 128:(sslot + 1) * 128], lhsT=kdTs[half][:, blk, :], rhs=vt[:, blk, :],
                                start=True, stop=True), reads=[('kdT', blk // 8), 'vt'], writes=[pkey])
                        if cnt > 0:
                            P.op('tensor', lambda e, oi=oi, c0_=j * 128 + half * 64, n=n, prev=prev, last=(ci == 1): e.matmul(
                                po[oi][:, c0_:c0_ + 64], lhsT=Sbf[prev][:], rhs=qe[:, n * 64:(n + 1) * 64], start=False, stop=last),
                                reads=[('Sbf', prev), ('qe', n // 16)], writes=[('po', oi)])
                        P.op('vector', lambda e, cur=cur, prev=prev, n=n, pst_t=pst_t, sslot=sslot: e.scalar_tensor_tensor(
                            out=S32[cur][:], in0=S32[prev][:], scalar=egl[:, n:n + 1], in1=pst_t[:, sslot * 128:(sslot + 1) * 128], op0=ALU.mult, op1=ALU.add),
                            reads=[('S32', prev), ('egl', n // 16), pkey], writes=[('S32', cur)])
                        P.op('scalar', lambda e, cur=cur: e.activation(out=Sbf[cur][:], in_=S32[cur][:], func=AF.Copy), reads=[('S32', cur)], writes=[('Sbf', cur)])
                        cnt += 1
                t0 = b0 * 128
                if d_ == first_dir:
                    P.op('scalar', lambda e, oi=oi, t0=t0: e.activation(out=O[:, t0:t0 + 512], in_=po[oi][:], func=AF.Copy), reads=[('po', oi)], writes=['O'])
                else:
                    P.op('vector', lambda e, oi=oi, t0=t0: e.tensor_tensor(out=O[:, t0:t0 + 512], in0=po[oi][:], in1=O[:, t0:t0 + 512], op=ALU.add),
                         reads=[('po', oi), 'O'], writes=['O'])
        if stage < 8:
            continue
        P.op('sync', lambda e, hd=hd: e.dma_start(out=qs[:], in_=gs_d[hd * 128:(hd + 1) * 128, :]), writes=['qs'], dma='qs')
        for pc in range(S // 512):
            t0 = pc * 512
            sq = self.sqb[pc % 2]
            psn, pkn = self.next_ps()
            P.op('scalar', lambda e, sq=sq, t0=t0: e.activation(out=sq[:], in_=O[:, t0:t0 + 512], func=AF.Square), reads=['O'], writes=[('sq', pc % 2)])
            P.op('tensor', lambda e, sq=sq, psn=psn: e.matmul(psn[:], lhsT=self.onesb[:], rhs=sq[:], start=True, stop=True), reads=[('sq', pc % 2), 'onesb'], writes=[pkn])
            P.op('scalar', lambda e, psn=psn: e.activation(out=self.rstd[:], in_=psn[:], func=AF.Ln, scale=1.0 / 128.0, bias=self.epsb[:, 0:1]),
                 reads=[pkn, 'epsb'], writes=['rstd'])
            P.op('scalar', lambda e: e.activation(out=self.rstd[:], in_=self.rstd[:], func=AF.Exp, scale=-0.5), reads=['rstd'], writes=['rstd'])
            P.op('vector', lambda e, t0=t0: e.scalar_tensor_tensor(out=tmpn[:], in0=O[:, t0:t0 + 512], scalar=self.cv[:, og:og + 1], in1=self.rstd[:],
                                                                   op0=ALU.mult, op1=ALU.mult), reads=['O', 'cv', 'rstd'], writes=['tmpn'])
            yi = ycnt % 2
            ycnt += 1
            P.op('gpsimd', lambda e, yi=yi, t0=t0: e.tensor_tensor(out=yst[yi][:], in0=tmpn[:], in1=qs[:, t0:t0 + 512], op=ALU.mult),
                 reads=['tmpn', 'qs'], writes=[('yst', yi)])
            P.op('sync', lambda e, yi=yi, t0=t0, hd=hd: e.dma_start(out=yd[512 + hd * 128:512 + (hd + 1) * 128, t0:t0 + 512], in_=yst[yi][:]),
                 reads=[('yst', yi)], writes=[y_out], dma=('yst', yi), group='yst')
    return P.end()


Builder.hgrn_phase = _hgrn_phase


_CACHE = {}
W_NAMES = ['ev_w_in', 'ev_w_out', 'od_w_in', 'od_w_out', 'xa_wq', 'xa_wkv', 'xa_wo', 'mlp_w1', 'mlp_w2', 'lru_wa', 'lru_wx']
C_NAMES = ['ropeC', 'ropeS', 'ropeR', 'maskA', 'maskNA', 'ebraw', 'trim', 'ident']


def build_full():
    B = Builder(ext_in=['xT'], ext_out=['outT'])
    stats = []
    stats.append(B.prep_phase())
    stats.append(B.qkv0_phase('xT'))
    stats.append(B.attA_phase('y0T'))
    stats.append(B.attB_phase('y0T'))
    stats.append(B.dense_phase(0, 'xT', 'y0T', 'xl0T', False))
    stats.append(B.proj1_phase('xl0T'))
    stats.append(B.lru_phase('y1T'))
    stats.append(B.hgrn_phase('y1T'))
    stats.append(B.dense_phase(1, 'xl0T', 'y1T', 'outT', True))
    return B, stats


def kernel(**inputs):
    inp = {k: np.asarray(v) for k, v in inputs.items()}
    if 'B' not in _CACHE:
        _CACHE['B'], _CACHE['stats'] = build_full()
    B = _CACHE['B']
    n = 8
    shared = {k: np.ascontiguousarray(inp[k], dtype=np.float32) for k in W_NAMES}
    shared['cvec'] = pack_cvec(inp)
    cs = make_consts(inp)
    for k in C_NAMES:
        shared[k] = np.ascontiguousarray(cs[k], dtype=np.float32)
    in_maps = []
    for b in range(n):
        m = dict(shared)
        m['xT'] = np.ascontiguousarray(inp['x'][b].T, dtype=np.float32)
        m['memT'] = np.ascontiguousarray(inp['mem'][b].T, dtype=np.float32)
        in_maps.append(m)
    res = run_bass_kernel_spmd(B.nc, in_maps, core_ids=list(range(n)))
    out = np.stack([np.ascontiguousarray(np.asarray(r['outT']).T) for r in res.results], axis=0)
    return out.astype(np.float32)
```
